# Optimizing a Trainium2 kernel written in Bass

```python
import jax, jax.numpy as jnp
from jax import lax
import numpy as np

D_MODEL = 1024
BATCH = 16
SEQ = 2048
DEPTH = 4

SSM_EXPAND = 2
SSM_WIDTH = SSM_EXPAND * D_MODEL
SSM_HEAD_DIM = 64
SSM_HEADS = SSM_WIDTH // SSM_HEAD_DIM
SSM_GROUPS = 2
SSM_STATE = 128
CONV_WIDTH = 4
CHUNK = 128
CONV_CH = SSM_WIDTH + 2 * SSM_GROUPS * SSM_STATE
POOL_WIDTH = D_MODEL
POOL_WINDOWS = (2, 4, 8, 16)
POOL_GROUPS = 4
POOL_GROUP_DIM = POOL_WIDTH // POOL_GROUPS
SB_WIDTH = D_MODEL
SB_HEAD_DIM = 64
SB_HEADS = SB_WIDTH // SB_HEAD_DIM
SB_BLOCK = 128
N_BRANCHES = 3
EPS = 1e-6
IN_SIZES = (SSM_WIDTH, CONV_CH, SSM_HEADS, POOL_WIDTH, POOL_WIDTH, 3 * SB_WIDTH, SB_WIDTH, N_BRANCHES * D_MODEL)
IN_COLS = SSM_WIDTH + CONV_CH + SSM_HEADS + 2 * POOL_WIDTH + 4 * SB_WIDTH + N_BRANCHES * D_MODEL

kernel_name = "hybrid_ssd_pool_stickbreak_gated_block"


def _split_points():
    pts, run = [], 0
    for s in IN_SIZES[:-1]:
        run += s
        pts.append(run)
    return pts


def rms_norm(x, w):
    xf = x.astype(jnp.float32)
    var = jnp.mean(xf * xf, axis=-1, keepdims=True)
    return (xf * lax.rsqrt(var + EPS)).astype(x.dtype) * w


def causal_dwconv(u, w, b):
    S = u.shape[1]
    up = jnp.pad(u, ((0, 0), (CONV_WIDTH - 1, 0), (0, 0)))
    out = b
    for k in range(CONV_WIDTH):
        out = out + up[:, k:k + S] * w[k]
    return out


def segsum(a):
    T = a.shape[-1]
    cs = jnp.cumsum(a, axis=-1)
    seg = cs[..., :, None] - cs[..., None, :]
    mask = jnp.tril(jnp.ones((T, T), dtype=bool))
    return jnp.where(mask, seg, -jnp.inf)


def ssd_chunked(xh, dt, a, Bg, Cg):
    Bsz, S, H, P = xh.shape
    G, N = Bg.shape[2], Bg.shape[3]
    hpg = H // G
    nc = S // CHUNK
    xdt = (xh * dt[..., None]).reshape(Bsz, nc, CHUNK, G, hpg, P)
    adt = (dt.astype(jnp.float32) * a.astype(jnp.float32)).reshape(Bsz, nc, CHUNK, G, hpg)
    adt = jnp.moveaxis(adt, 2, -1)
    Bc = Bg.reshape(Bsz, nc, CHUNK, G, N)
    Cc = Cg.reshape(Bsz, nc, CHUNK, G, N)
    a_cum = jnp.cumsum(adt, axis=-1)
    decay_in = jnp.exp(segsum(adt))
    cb = jnp.einsum('bclgn,bcsgn->bcgls', Cc, Bc)
    y_diag = jnp.einsum('bcghls,bcsghp->bclghp', cb[:, :, :, None] * decay_in, xdt)
    decay_states = jnp.exp(a_cum[..., -1:] - a_cum)
    states = jnp.einsum('bclgn,bcghl,bclghp->bcghpn', Bc, decay_states, xdt)
    chunk_decay = jnp.exp(a_cum[..., -1])

    def step(carry, inp):
        st, dec = inp
        return carry * dec[..., None, None] + st, carry

    init = jnp.zeros_like(states[:, 0])
    _, prev = lax.scan(step, init, (jnp.moveaxis(states, 1, 0), jnp.moveaxis(chunk_decay, 1, 0)))
    prev = jnp.moveaxis(prev, 0, 1)
    y_off = jnp.einsum('bclgn,bcghpn,bcghl->bclghp', Cc, prev, jnp.exp(a_cum))
    return (y_diag + y_off).reshape(Bsz, S, H, P)


def mamba2_branch(z, xbc, dt_raw, conv_w, conv_b, dt_bias, a_log, d_skip, ssm_norm_w):
    Bsz, S, _ = z.shape
    xbc = jax.nn.silu(causal_dwconv(xbc, conv_w, conv_b))
    xs, Bg, Cg = jnp.split(xbc, [SSM_WIDTH, SSM_WIDTH + SSM_GROUPS * SSM_STATE], axis=-1)
    xh = xs.reshape(Bsz, S, SSM_HEADS, SSM_HEAD_DIM)
    Bg = Bg.reshape(Bsz, S, SSM_GROUPS, SSM_STATE)
    Cg = Cg.reshape(Bsz, S, SSM_GROUPS, SSM_STATE)
    dt = jax.nn.softplus((dt_raw + dt_bias).astype(jnp.float32))
    a = -jnp.exp(a_log.astype(jnp.float32))
    y = ssd_chunked(xh, dt, a, Bg, Cg) + xh * d_skip[:, None]
    y = y.reshape(Bsz, S, SSM_WIDTH).astype(z.dtype)
    return rms_norm(y * jax.nn.silu(z), ssm_norm_w)


def pool_branch(u, gate, pool_w, pool_scale):
    Bsz, S, _ = u.shape
    uf = u.astype(jnp.float32).reshape(Bsz, S, POOL_GROUPS, POOL_GROUP_DIM)
    cs = jnp.cumsum(uf, axis=1)
    pos = jnp.arange(S)
    pooled = []
    for g, w in enumerate(POOL_WINDOWS):
        csw = jnp.pad(cs[:, :, g], ((0, 0), (w, 0), (0, 0)))
        win_sum = csw[:, w:] - csw[:, :S]
        cnt = jnp.minimum(pos + 1, w).astype(jnp.float32)
        pooled.append(win_sum / cnt[None, :, None])
    mixed = jnp.stack(pooled, axis=2) - uf
    mixed = jnp.einsum('bsgi,gio->bsgo', mixed.astype(u.dtype), pool_w).reshape(Bsz, S, POOL_WIDTH)
    return (mixed * pool_scale * jax.nn.silu(gate)).astype(u.dtype)


def stick_breaking_branch(qkv, gate):
    Bsz, S, _ = qkv.shape
    q, k, v = jnp.split(qkv, 3, axis=-1)

    def heads(t):
        return t.reshape(Bsz, S, SB_HEADS, SB_HEAD_DIM).transpose(0, 2, 1, 3)

    q, k, v = heads(q), heads(k), heads(v)
    scale = SB_HEAD_DIM ** -0.5
    outs = []
    for i in range(S // SB_BLOCK):
        q0 = i * SB_BLOCK
        kend = q0 + SB_BLOCK
        qb = q[:, :, q0:kend]
        kb = k[:, :, :kend]
        vb = v[:, :, :kend]
        z = jnp.einsum('bhtd,bhsd->bhts', qb, kb).astype(jnp.float32) * scale
        causal = (q0 + jnp.arange(SB_BLOCK))[:, None] > jnp.arange(kend)[None, :]
        log_beta = jax.nn.log_sigmoid(z)
        log_one_minus = jnp.where(causal, jax.nn.log_sigmoid(-z), 0.0)
        later = lax.cumsum(log_one_minus, axis=3, reverse=True) - log_one_minus
        att = jnp.where(causal, jnp.exp(log_beta + later), 0.0)
        outs.append(jnp.einsum('bhts,bhsd->bhtd', att.astype(vb.dtype), vb))
    o = jnp.concatenate(outs, axis=2).transpose(0, 2, 1, 3).reshape(Bsz, S, SB_WIDTH)
    return o * jax.nn.silu(gate)


def hybrid_layer(x, norm_w, w_in, conv_w, conv_b, dt_bias, a_log, d_skip, ssm_norm_w,
                 pool_w, pool_scale, w_proj_ssm, w_proj_pool, w_proj_sb, w_out):
    Bsz, S, D = x.shape
    h = rms_norm(x, norm_w)
    proj = h @ w_in
    z, xbc, dt_raw, pool_u, pool_gate, qkv, sb_gate, merge = jnp.split(proj, _split_points(), axis=-1)
    y_ssm = mamba2_branch(z, xbc, dt_raw, conv_w, conv_b, dt_bias, a_log, d_skip, ssm_norm_w) @ w_proj_ssm
    y_pool = pool_branch(pool_u, pool_gate, pool_w, pool_scale) @ w_proj_pool
    y_sb = stick_breaking_branch(qkv, sb_gate) @ w_proj_sb
    g = jax.nn.sigmoid(merge.astype(jnp.float32)).reshape(Bsz, S, N_BRANCHES, D).astype(x.dtype)
    merged = g[:, :, 0] * y_ssm + g[:, :, 1] * y_pool + g[:, :, 2] * y_sb
    return x + (merged @ w_out).astype(x.dtype)


def setup_inputs(seed: int = 0) -> dict:
    key = jax.random.key(seed)
    ks = jax.random.split(key, 17)
    f32 = jnp.float32
    nrm = lambda k, shape, s: jax.random.normal(k, shape, f32) * s
    dt = jnp.exp(jax.random.uniform(ks[5], (DEPTH, SSM_HEADS), f32, float(np.log(1e-3)), float(np.log(1e-1))))
    return {
        "x": nrm(ks[0], (BATCH, SEQ, D_MODEL), 1.0),
        "norm_w": 1.0 + nrm(ks[1], (DEPTH, D_MODEL), 0.02),
        "w_in": nrm(ks[2], (DEPTH, D_MODEL, IN_COLS), D_MODEL ** -0.5),
        "conv_w": nrm(ks[3], (DEPTH, CONV_WIDTH, CONV_CH), CONV_WIDTH ** -0.5),
        "conv_b": nrm(ks[4], (DEPTH, CONV_CH), 0.02),
        "dt_bias": dt + jnp.log(-jnp.expm1(-dt)),
        "a_log": jnp.log(jax.random.uniform(ks[6], (DEPTH, SSM_HEADS), f32, 1.0, 16.0)),
        "d_skip": 1.0 + nrm(ks[7], (DEPTH, SSM_HEADS), 0.02),
        "ssm_norm_w": 1.0 + nrm(ks[8], (DEPTH, SSM_WIDTH), 0.02),
        "pool_w": nrm(ks[9], (DEPTH, POOL_GROUPS, POOL_GROUP_DIM, POOL_GROUP_DIM), POOL_GROUP_DIM ** -0.5),
        "pool_scale": 1.0 + nrm(ks[10], (DEPTH, POOL_WIDTH), 0.02),
        "w_proj_ssm": nrm(ks[11], (DEPTH, SSM_WIDTH, D_MODEL), SSM_WIDTH ** -0.5),
        "w_proj_pool": nrm(ks[12], (DEPTH, POOL_WIDTH, D_MODEL), POOL_WIDTH ** -0.5),
        "w_proj_sb": nrm(ks[13], (DEPTH, SB_WIDTH, D_MODEL), SB_WIDTH ** -0.5),
        "w_out": nrm(ks[14], (DEPTH, D_MODEL, D_MODEL), (N_BRANCHES * D_MODEL) ** -0.5),
        "final_norm_w": 1.0 + nrm(ks[15], (D_MODEL,), 0.02),
    }


def reference(x, norm_w, w_in, conv_w, conv_b, dt_bias, a_log, d_skip, ssm_norm_w,
              pool_w, pool_scale, w_proj_ssm, w_proj_pool, w_proj_sb, w_out, final_norm_w):
    for l in range(DEPTH):
        x = hybrid_layer(x, norm_w[l], w_in[l], conv_w[l], conv_b[l], dt_bias[l], a_log[l],
                         d_skip[l], ssm_norm_w[l], pool_w[l], pool_scale[l], w_proj_ssm[l],
                         w_proj_pool[l], w_proj_sb[l], w_out[l])
    return rms_norm(x, final_norm_w)
```

```python
import contextlib
import numpy as np
import concourse.bass as bass
import concourse.mybir as mybir
from concourse.bass_utils import run_bass_kernel_spmd

F32 = mybir.dt.float32
BF16 = mybir.dt.bfloat16
AF = mybir.ActivationFunctionType
ALU = mybir.AluOpType

D = 1024
SSMW = 2048
NH = 32
CONVC = 2560
INC = 13856
C_Z, C_XBC, C_DT, C_PU, C_PG, C_Q, C_K, C_V, C_SG, C_MG = 0, 2048, 4608, 4640, 5664, 6688, 7712, 8736, 9760, 10784
EPS = 1e-6
TT = 512
SAME_ENGINE_SYNC = True
NO_SELF_SYNC = ("pe",)


class Buf:
    __slots__ = ("w", "r", "sem", "name")

    def __init__(self, name=""):
        self.w = None
        self.r = []
        self.sem = None
        self.name = name


class SemH:
    def __init__(self, h, name):
        self.h = h
        self.name = name
        self.total = 0


class Plan:
    ENGS = ("pe", "dve", "act", "pool", "sp")

    def __init__(self, nc, stack):
        self.nc = nc
        self.stack = stack
        self.streams = {e: [] for e in self.ENGS}
        self.cnt = {e: 0 for e in self.ENGS}
        self.waited = {e: {} for e in self.ENGS}
        self.nsem = 0
        self.esem = {e: self.new_sem("e_" + e) for e in self.ENGS}
        self.ninst = 0
        self.semcache = {}

    def new_sem(self, name):
        self.nsem += 1
        return SemH(self.stack.enter_context(self.nc.semaphore(name)), name)

    def _wait(self, eng, tickets):
        need = {}
        for (sem, val, isdma) in tickets:
            if isdma:
                val = sem.total
            elif sem is self.esem[eng] and (eng in NO_SELF_SYNC or not SAME_ENGINE_SYNC):
                continue
            if need.get(sem, 0) < val:
                need[sem] = val
        for sem, val in need.items():
            if self.waited[eng].get(sem, 0) < val:
                self.streams[eng].append(("w", sem, val))
                self.waited[eng][sem] = val

    def _deps(self, reads, writes):
        t = []
        for b in reads:
            if b.w is not None:
                t.append(b.w)
        for b in writes:
            if b.w is not None:
                t.append(b.w)
            t.extend(b.r)
        return t

    def op(self, eng, fn, reads=(), writes=()):
        ex = [b for b in reads if b.name.startswith("bank")]
        if ex:
            reads = [b for b in reads if not b.name.startswith("bank")]
            writes = list(writes) + ex
        self._wait(eng, self._deps(reads, writes))
        sem = self.esem[eng]
        self.cnt[eng] += 1
        sem.total = self.cnt[eng]
        tk = (sem, self.cnt[eng], False)
        self.streams[eng].append(("i", fn, sem, 1))
        for b in reads:
            b.r.append(tk)
        for b in writes:
            b.w = tk
            b.r = []
        self.ninst += 1

    def dma(self, eng, out_ap, in_ap, reads, writes, owner):
        self._wait(eng, self._deps(reads, writes))
        if owner.sem is None:
            if owner.name not in self.semcache:
                self.semcache[owner.name] = self.new_sem("d_" + owner.name)
            owner.sem = self.semcache[owner.name]
        sem = owner.sem
        sem.total += 16
        tk = (sem, sem.total, True)
        self.streams[eng].append(("i", lambda e: e.dma_start(out=out_ap, in_=in_ap), sem, 16))
        for b in reads:
            b.r.append(tk)
        for b in writes:
            b.w = tk
            b.r = []
        self.ninst += 1

    def final_wait(self, eng, bufs):
        t = []
        for b in bufs:
            if b.w is not None:
                t.append(b.w)
            t.extend(b.r)
        self._wait(eng, t)

    def emit(self):
        nc = self.nc
        streams = self.streams

        def replay(name, e):
            for rec in streams[name]:
                if rec[0] == "w":
                    e.wait_ge(rec[1].h, rec[2])
                else:
                    rec[1](e).then_inc(rec[2].h, rec[3])

        with nc.Block() as block:
            @block.tensor
            def _(e):
                replay("pe", e)

            @block.vector
            def _(e):
                replay("dve", e)

            @block.scalar
            def _(e):
                replay("act", e)

            @block.gpsimd
            def _(e):
                replay("pool", e)

            @block.sync
            def _(e):
                replay("sp", e)


def pipeline(items, stages, order=None, hook=None):
    n, d = len(items), len(stages)
    order = list(reversed(range(d))) if order is None else order
    for step in range(n + d - 1):
        if hook is not None:
            hook(step)
        for si in order:
            st = stages[si]
            k = step - si
            if 0 <= k < n:
                st(items[k])


def host_consts():
    i = np.arange(128)
    c = {}
    c["c_ident"] = np.eye(128, dtype=np.float32)
    c["c_U"] = (i[:, None] <= i[None, :]).astype(np.float32)
    c["c_Ls"] = (i[:, None] > i[None, :]).astype(np.float32)
    c["c_ones"] = np.ones((128, 128), np.float32)
    c["c_nLi"] = -(i[:, None] >= i[None, :]).astype(np.float32)
    c["c_nOnes"] = -np.ones((128, 128), np.float32)
    c["c_NEG"] = -30000.0 * (i[:, None] >= i[None, :]).astype(np.float32)
    inv = np.zeros((128, 4, 16), np.float32)
    for g, w in enumerate((2, 4, 8, 16)):
        inv[:, g, :] = 1.0 / np.minimum(np.arange(16) + 1, w)
    c["c_invcnt"] = inv.reshape(128, 64)
    return c


def build(S=2048, NSEQ=2, NL=4, final=True, taps=()):
    nc = bass.Bass("TRN2", target_bir_lowering=False)
    NT = NSEQ * S
    NTT = S // TT
    stack = contextlib.ExitStack()
    with stack:
        pl = Plan(nc, stack)

        def dram(name, shape, dt=F32, kind="ExternalInput"):
            return nc.dram_tensor(name, list(shape), dt, kind=kind).ap()

        x_in = dram("x", [NT, D])
        y_out = dram("out", [NT, D], kind="ExternalOutput")
        w_in = dram("w_in", [NL, D, INC])
        w_ps = dram("w_ps", [NL, SSMW, D])
        w_pp = dram("w_pp", [NL, D, D])
        w_pb = dram("w_pb", [NL, D, D])
        w_out = dram("w_out", [NL, D, D])
        pool_w = dram("pool_w", [NL, 4, 256, 256])
        p_normw = dram("p_normw", [NL, 128, D])
        p_convw = dram("p_convw", [NL, 128, 80])
        p_convb = dram("p_convb", [NL, 128, 20])
        p_dtb = dram("p_dtb", [NL, 128, 32])
        p_alog = dram("p_alog", [NL, 128, 32])
        p_dskip = dram("p_dskip", [NL, 128, 16])
        p_ssmnw = dram("p_ssmnw", [NL, 128, 16])
        p_pscale = dram("p_pscale", [NL, 128, 8])
        p_fnw = dram("p_fnw", [128, D])
        cdr = {k: dram(k, v.shape) for k, v in host_consts().items()}
        xscr = [dram("xscr%d" % i, [NT, D], kind="Internal") for i in range(2)] if NL > 1 else []
        tapd = {}

        def sb(name, shape, dt):
            return stack.enter_context(nc.sbuf_tensor(name, list(shape), dt))

        cst_f = {k: sb("f" + k, [128, 128], F32) for k in ("c_U", "c_Ls", "c_ones")}
        cst_b = {k: sb("b" + k, [128, 128], BF16) for k in ("c_ident", "c_ones", "c_nLi", "c_nOnes", "c_NEG")}
        invcnt = sb("invcnt", [128, 4, 16], F32)
        B_const = Buf("const")
        for k, t in cst_f.items():
            pl.dma("sp", t[:], cdr[k], [], [B_const], B_const)
        for k, t in cst_b.items():
            pl.dma("pool", t[:], cdr[k], [], [B_const], B_const)
        pl.dma("sp", invcnt[:].rearrange("p a b -> p (a b)"), cdr["c_invcnt"], [], [B_const], B_const)
        U_f, Ls_f, ones_f = cst_f["c_U"], cst_f["c_Ls"], cst_f["c_ones"]
        ident_b, ones_b, nLi_b, nOnes_b, NEG_b = (cst_b[k] for k in ("c_ident", "c_ones", "c_nLi", "c_nOnes", "c_NEG"))

        normw = sb("normw", [128, D], F32)
        convw = sb("convw", [128, 20, 4], F32)
        convb = sb("convb", [128, 20], F32)
        dtb = sb("dtb", [128, 32], F32)
        a_neg = sb("a_neg", [128, 32], F32)
        dskip = sb("dskip", [128, 16], F32)
        ssmnw = sb("ssmnw", [128, 16], F32)
        inv_nw = sb("inv_nw", [128, 16], F32)
        pscale = sb("pscale", [128, 8], F32)
        B_par = Buf("par")

        Kc = sb("Kc", [128, 8, S], BF16)
        Vc = sb("Vc", [128, S // 128, D], BF16)
        ST = sb("ST", [128, 2048], F32)
        STb = sb("STb", [128, 2048], BF16)
        chalo = sb("chalo", [128, 20, 3], F32)
        B_Kc = [Buf("Kc%d" % i) for i in range(NTT)]
        B_Vc = [Buf("Vc%d" % i) for i in range(NTT)]
        phalo = sb("phalo", [128, 8, 16], F32)
        B_ST = [Buf("ST%d" % i) for i in range(4)]
        B_STb = [Buf("STb%d" % i) for i in range(4)]
        B_chalo = [Buf("chalo%d" % i) for i in range(20)]
        B_phalo = [Buf("phalo%d" % i) for i in range(8)]

        sst = sb("sst", [128, 8], F32)
        B_sst = [Buf("sst%d" % i) for i in range(2)]
        hT = sb("hT", [128, 8, TT], BF16)
        B_hT = Buf("hT")
        NSLOT = 3
        wsl = [sb("wsl%d" % i, [128, 2048], BF16) for i in range(NSLOT)]
        B_wsl = [Buf("wsl%d" % i) for i in range(NSLOT)]
        B_mg = [Buf("mg%d" % i) for i in range(8)]
        gate = [sb("gate%d" % i, [128, TT], BF16) for i in range(2)]
        B_gate = [Buf("gate%d" % i) for i in range(2)]
        t1 = [sb("t1_%d" % i, [128, TT], F32) for i in range(2)]
        B_t1 = [Buf("t1_%d" % i) for i in range(2)]
        ARENA_F = 24064
        MG_OFF = ARENA_F - 8 * TT
        arena = sb("arena", [128, ARENA_F], F32)
        mg = arena[:, MG_OFF:ARENA_F].rearrange("p (a b) -> p a b", a=8)
        ar = {"off": 0, "bufs": [], "fence": [], "limit": MG_OFF}

        def all_tickets(bufs):
            f = []
            for b in bufs:
                if b.w is not None:
                    f.append(b.w)
                f.extend(b.r)
            best = {}
            for (s_, v, d_) in f:
                if s_ not in best or best[s_][1] < v:
                    best[s_] = (s_, v, d_)
            return list(best.values())

        class _F:
            pass

        def ar_reset(limit=None):
            fb = _F()
            fb.w = None
            fb.r = list(ar["fence"])
            ar["fence"] = all_tickets([fb] + ar["bufs"] + B_mg)
            ar["bufs"] = []
            ar["off"] = 0
            ar["limit"] = MG_OFF if limit is None else limit

        def ar_alloc(name, shape, dt, nbuf=1):
            n = int(np.prod(shape[1:]))
            words = n if dt == F32 else (n + 1) // 2
            o = ar["off"]
            assert o + words <= ar["limit"], ("arena overflow", name, o, words, ar["limit"])
            ar["off"] = o + words
            ap = arena[:, o:o + words]
            if dt != F32:
                ap = ap.bitcast(BF16)
                if n % 2:
                    ap = ap[:, 0:n]
            if len(shape) == 3:
                ap = ap.rearrange("p (a b) -> p a b", a=shape[1])
            elif len(shape) == 4:
                ap = ap.rearrange("p (a b c) -> p a b c", a=shape[1], b=shape[2])
            bufs = []
            for i in range(nbuf):
                b = Buf(name + str(i))
                b.r = list(ar["fence"])
                ar["bufs"].append(b)
                bufs.append(b)
            return ap, bufs

        ps_all = stack.enter_context(nc.psum_tensor("ps_all", [128, 8, 512], F32))
        banks = [ps_all[:, i, :] for i in range(8)]
        B_bank = [Buf("bank%d" % i) for i in range(8)]
        rr = {"i": 0}

        def nbank(pool=(0, 1, 2, 3, 4, 5, 6, 7)):
            i = pool[rr["i"] % len(pool)]
            rr["i"] += 1
            return i

        slot_i = {"i": 0}

        def load_w(src, KC, C):
            si = slot_i["i"] % NSLOT
            slot_i["i"] += 1
            view = wsl[si][:, 0:KC * C].rearrange("p (k c) -> p k c", k=KC)
            pl.dma("pool", view, src.rearrange("(k p) c -> p k c", p=128), [], [B_wsl[si]], B_wsl[si])
            return view, B_wsl[si]

        def mm(out_ap, lhsT, rhs, start, stop, reads, bank, **kw):
            pl.op("pe", lambda e: e.matmul(out_ap, lhsT, rhs, start=start, stop=stop, **kw), reads, [B_bank[bank]])

        def act(out_ap, in_ap, func, reads, writes, **kw):
            pl.op("act", lambda e: e.activation(out=out_ap, in_=in_ap, func=func, **kw), reads, writes)

        def tt_(eng, out_ap, in0, in1, op, reads, writes):
            pl.op(eng, lambda e: e.tensor_tensor(out=out_ap, in0=in0, in1=in1, op=op), reads, writes)

        def ts_(eng, out_ap, in0, s1, s2, op0, op1, reads, writes):
            if s2 is None:
                pl.op(eng, lambda e: e.tensor_scalar(out=out_ap, in0=in0, scalar1=s1, scalar2=None, op0=op0), reads, writes)
            else:
                pl.op(eng, lambda e: e.tensor_scalar(out=out_ap, in0=in0, scalar1=s1, scalar2=s2, op0=op0, op1=op1), reads, writes)

        def stt_(eng, out_ap, in0, scalar, in1, op0, op1, reads, writes):
            pl.op(eng, lambda e: e.scalar_tensor_tensor(out=out_ap, in0=in0, scalar=scalar, in1=in1, op0=op0, op1=op1), reads, writes)

        def cp_(eng, out_ap, in_ap, reads, writes):
            pl.op(eng, lambda e: e.tensor_copy(out=out_ap, in_=in_ap), reads, writes)

        def mset(eng, ap, val, writes):
            pl.op(eng, lambda e: e.memset(ap, val), [], writes)

        def transpose(out_ap, in_ap, reads, bank):
            pl.op("pe", lambda e: e.transpose(out_ap, in_ap, ident_b[:]), list(reads) + [B_const], [B_bank[bank]])

        def proj_fm(wsrc, KC, ncols, rhs_fn, rhs_reads, evac, mm_pool=(0, 1, 2, 3), joint=False):
            wv, wb = load_w(wsrc, KC, ncols)
            done = []
            for m in range(ncols // 128):
                bk = nbank(mm_pool)
                for kc in range(KC):
                    mm(banks[bk][:, :], wv[:, kc, m * 128:(m + 1) * 128], rhs_fn(kc), kc == 0, kc == KC - 1,
                       [wb] + list(rhs_reads), bk)
                if joint:
                    done.append((m, bk))
                else:
                    evac(m, bk)
            if joint:
                evac(done)

        def tap(name, ap_sb, buf, shape):
            if name in taps:
                if name not in tapd:
                    tapd[name] = (dram("tap_" + name, shape, ap_sb.dtype, kind="ExternalOutput"), Buf("tap_" + name))
                pl.dma("sp", tapd[name][0], ap_sb, [buf], [tapd[name][1]], buf)

        B_xd = {}

        def xbuf(which, r0):
            return B_xd.setdefault((which, r0), Buf("xd"))

        outs_final = []

        for l in range(NL):
            src_x, src_id = (x_in, "in") if l == 0 else (xscr[(l - 1) % 2], "s%d" % ((l - 1) % 2))
            last = (l == NL - 1)
            dst_x, dst_id = (y_out, "out") if last else (xscr[l % 2], "s%d" % (l % 2))
            for (t, srcp) in ((normw, p_normw[l]), (convb, p_convb[l]), (dtb, p_dtb[l]), (a_neg, p_alog[l]),
                              (dskip, p_dskip[l]), (ssmnw, p_ssmnw[l]), (pscale, p_pscale[l])):
                pl.dma("sp", t[:], srcp, [], [B_par], B_par)
            pl.dma("sp", convw[:].rearrange("p a b -> p (a b)"), p_convw[l], [], [B_par], B_par)
            act(a_neg[:], a_neg[:], AF.Exp, [B_par], [B_par])
            ts_("dve", a_neg[:], a_neg[:], -1.0, None, ALU.mult, None, [B_par], [B_par])
            pl.op("dve", lambda e: e.reciprocal(out=inv_nw[:], in_=ssmnw[:]), [B_par], [B_par])

            prefetched = {"v": False}
            for sq in range(NSEQ):
                mset("dve", ST[:], 0.0, B_ST)
                mset("dve", STb[:], 0.0, B_STb)
                mset("dve", chalo[:], 0.0, B_chalo)
                for tt in range(NTT):
                    tok0 = sq * S + tt * TT
                    first_tile = (tt == 0)
                    def norm_block(tokb, j, xblk_a, B_xblk, hb_a, B_hb, tpool=(6, 7)):
                        r0 = tokb + j * 128
                        xb, Bx = xblk_a[:, j % 2, :], B_xblk[j % 2]
                        hbj, Bh = hb_a[:, j % 2, :], B_hb[j % 2]
                        ssj, Bss = sst[:, (j % 2) * 4:(j % 2) * 4 + 1], B_sst[j % 2]
                        rsj = sst[:, (j % 2) * 4 + 1:(j % 2) * 4 + 2]
                        pl.dma("sp", xb, src_x[r0:r0 + 128, :], [xbuf(src_id, r0)], [Bx], Bx)
                        mset("dve", ssj, 0.0, [Bss])
                        act(hbj, xb, AF.Square, [Bx, Bss], [Bh, Bss], accum_out=ssj)
                        act(rsj, ssj, AF.Ln, [Bss], [Bss], scale=1.0 / D, bias=EPS)
                        act(rsj, rsj, AF.Exp, [Bss], [Bss], scale=-0.5)
                        stt_("dve", hbj, xb, rsj, normw[:], ALU.mult, ALU.mult, [Bx, Bss, B_par], [Bh])
                        bk = nbank(tpool)
                        pbf = banks[bk][:, :].bitcast(BF16)
                        for kc in range(8):
                            transpose(pbf[:, kc * 128:(kc + 1) * 128], hbj[:, kc * 128:(kc + 1) * 128], [Bh], bk)
                        cp_("dve", hT[:, :, j * 128:(j + 1) * 128], pbf.rearrange("p (k t) -> p k t", k=8),
                            [B_bank[bk]], [B_hT])

                    if not prefetched["v"]:
                        ar_reset()
                        xblk_a, B_xblk = ar_alloc("xblk", [128, 2, D], F32, 2)
                        hb_a, B_hb = ar_alloc("hb", [128, 2, D], BF16, 2)
                        for j in range(4):
                            norm_block(tok0, j, xblk_a, B_xblk, hb_a, B_hb)
                    prefetched["v"] = False
                    tap("hT", hT[:].rearrange("p k t -> p (k t)"), B_hT, [128, 8 * TT])

                    rhs_h = lambda kc: hT[:, kc, :]

                    def gate_block(mb, gi):
                        gbuf, Bg = gate[mb % 2], B_gate[mb % 2]
                        c0 = C_MG + gi * D + mb * 128

                        def ev(m, bk):
                            act(gbuf[:], banks[bk][:, :], AF.Sigmoid, [B_bank[bk]], [Bg])
                        proj_fm(w_in[l][:, c0:c0 + 128], 8, 128, rhs_h, [B_hT], ev)
                        return gbuf, Bg

                    ar_reset(ARENA_F)
                    zs, B_zs = ar_alloc("zs", [128, 16, TT], BF16, 16)
                    rstdb, B_rstdb = ar_alloc("rstdb", [128, TT], F32, 1)
                    xbcT, B_xbc = ar_alloc("xbcT", [128, 20, TT], BF16, 20)
                    upre, B_upre = ar_alloc("upre", [128, 2, TT + 4], F32, 2)
                    cacc, B_cacc = ar_alloc("cacc", [128, 2, TT], F32, 2)
                    dts, B_dts = ar_alloc("dts", [128, 8, 128], F32, 1)
                    B_dts = B_dts[0]
                    xdt, B_xdt = ar_alloc("xdt", [128, 2, 2048], BF16, 2)
                    xdtd, B_xdtd = ar_alloc("xdtd", [128, 2, 2048], BF16, 2)
                    Btok, B_Btok = ar_alloc("Btok", [128, 2, 256], BF16, 2)
                    cbTm, B_cbTm = ar_alloc("cbTm", [128, 2, 256], BF16, 2)
                    rseg, B_rseg = ar_alloc("rseg", [128, 2, 1024], F32, 2)
                    MT, B_MT = ar_alloc("MT", [128, 16, 128], BF16, 2)
                    ytmp, B_ytmp = ar_alloc("ytmp", [128, 2, 512], F32, 2)
                    ytok, B_ytok = ar_alloc("ytok", [128, 2, 2048], BF16, 8)
                    tA, B_tA = ar_alloc("tA", [128, 2, 128], F32, 2)
                    sq_, B_sq = ar_alloc("sq", [128, 2, 128], BF16, 2)
                    B_rstdb = B_rstdb[0]

                    wv, wb = load_w(w_in[l][:, C_DT:C_DT + 32], 8, 32)
                    bk = nbank((0, 1, 2, 3))
                    for j in range(4):
                        for kc in range(8):
                            mm(banks[bk][:, j * 32:(j + 1) * 32], hT[:, kc, j * 128:(j + 1) * 128], wv[:, kc, :],
                               kc == 0, kc == 7, [wb, B_hT], bk)
                    d3 = lambda i: dts[:, i, :].rearrange("p (j h) -> p j h", j=4)
                    bc4 = lambda t: t[:].unsqueeze(1).to_broadcast([128, 4, 32])
                    tt_("dve", d3(0), banks[bk][:, 0:128].rearrange("p (j h) -> p j h", j=4), bc4(dtb), ALU.add,
                        [B_bank[bk], B_par], [B_dts])
                    act(dts[:, 0, :], dts[:, 0, :], AF.Exp, [B_dts], [B_dts])
                    act(dts[:, 1, :], dts[:, 0, :], AF.Ln, [B_dts], [B_dts], bias=1.0)
                    tt_("dve", d3(2), d3(1), bc4(a_neg), ALU.mult, [B_dts, B_par], [B_dts])
                    def dt_part_b():
                        bk = nbank((0, 1, 2, 3))
                        mm(banks[bk][:, 0:128], U_f[:], dts[:, 2, :], True, True, [B_dts, B_const], bk)
                        mm(banks[bk][:, 128:256], ones_f[:], dts[:, 2, :], True, True, [B_dts, B_const], bk)
                        cp_("dve", dts[:, 3, :], banks[bk][:, 0:128], [B_bank[bk]], [B_dts])
                        tt_("dve", dts[:, 4, :], banks[bk][:, 128:256], dts[:, 3, :], ALU.subtract, [B_bank[bk], B_dts], [B_dts])
                        act(dts[:, 4, :], dts[:, 4, :], AF.Exp, [B_dts], [B_dts])
                        act(dts[:, 5, :], dts[:, 3, :], AF.Exp, [B_dts], [B_dts])
                        act(dts[:, 6, :], banks[bk][:, 128:256], AF.Exp, [B_bank[bk]], [B_dts])
                        tt_("dve", dts[:, 7, :], dts[:, 1, :], dts[:, 4, :], ALU.mult, [B_dts], [B_dts])
                        tap("dts", dts[:].rearrange("p k t -> p (k t)"), B_dts, [128, 1024])

                    def z_blk(blk):
                        def ev(m, bk, blk=blk):
                            cb = blk * 2 + m
                            act(zs[:, cb, :], banks[bk][:, :], AF.Silu, [B_bank[bk]], [B_zs[cb]])
                        proj_fm(w_in[l][:, C_Z + blk * 256:C_Z + (blk + 1) * 256], 8, 256, rhs_h, [B_hT], ev)
                    def x_blk(blk):
                        def ev(done, blk=blk):
                            cbs = [(blk * 2 + m, bk) for (m, bk) in done]
                            for cb, bk in cbs:
                                u, Bu = upre[:, cb % 2, :], B_upre[cb % 2]
                                ca, Bca = cacc[:, cb % 2, :], B_cacc[cb % 2]
                                act(u[:, 4:TT + 4], banks[bk][:, :], AF.Copy, [B_bank[bk]], [Bu])
                                act(ca, banks[bk][:, :], AF.Identity, [B_bank[bk], B_par], [Bca],
                                    scale=convw[:, cb, 3:4], bias=convb[:, cb:cb + 1])
                                cp_("dve", u[:, 1:4], chalo[:, cb, :], [B_chalo[cb]], [Bu])
                            for k in (2, 1, 0):
                                for cb, bk in cbs:
                                    u, Bu = upre[:, cb % 2, :], B_upre[cb % 2]
                                    ca, Bca = cacc[:, cb % 2, :], B_cacc[cb % 2]
                                    stt_("dve", ca, u[:, 1 + k:1 + k + TT], convw[:, cb, k:k + 1], ca, ALU.mult, ALU.add,
                                         [Bu, B_par, Bca], [Bca])
                            for cb, bk in cbs:
                                u, Bu = upre[:, cb % 2, :], B_upre[cb % 2]
                                ca, Bca = cacc[:, cb % 2, :], B_cacc[cb % 2]
                                cp_("dve", chalo[:, cb, :], u[:, TT + 1:TT + 4], [Bu], [B_chalo[cb]])
                                act(xbcT[:, cb, :], ca, AF.Silu, [Bca], [B_xbc[cb]])
                        proj_fm(w_in[l][:, C_XBC + blk * 256:C_XBC + (blk + 1) * 256], 8, 256, rhs_h, [B_hT], ev, joint=True)
                    for blk in range(10):
                        x_blk(blk)
                        if blk == 1:
                            dt_part_b()
                        if blk >= 2:
                            z_blk(blk - 2)
                    tap("zs", zs[:].rearrange("p k t -> p (k t)"), B_zs[15], [128, 16 * TT])
                    tap("xbcT", xbcT[:].rearrange("p k t -> p (k t)"), B_xbc[19], [128, 20 * TT])
                    ssbk = nbank((4, 5))

                    def hsl(c, i, h0, n):
                        return dts[:, i, c * 32 + h0:c * 32 + h0 + n]

                    def ssd_F_thunks(c):
                        p = c % 2
                        csl = slice(c * 128, (c + 1) * 128)

                        def xs_half(hh):
                            bk = nbank((6, 7))
                            pbf = banks[bk][:, :].bitcast(BF16)
                            for i in range(8):
                                cb = hh * 8 + i
                                transpose(pbf[:, i * 128:(i + 1) * 128], xbcT[:, cb, csl], [B_xbc[cb]], bk)
                            pv = pbf.rearrange("p (h d) -> p h d", d=64)
                            tt_("dve", xdt[:, p, hh * 1024:(hh + 1) * 1024].rearrange("p (h d) -> p h d", d=64), pv,
                                hsl(c, 1, hh * 16, 16).unsqueeze(2).to_broadcast([128, 16, 64]), ALU.mult,
                                [B_bank[bk], B_dts], [B_xdt[p]])
                            tt_("dve", xdtd[:, p, hh * 1024:(hh + 1) * 1024].rearrange("p (h d) -> p h d", d=64), pv,
                                hsl(c, 7, hh * 16, 16).unsqueeze(2).to_broadcast([128, 16, 64]), ALU.mult,
                                [B_bank[bk], B_dts], [B_xdtd[p]])

                        def bcb():
                            bk = nbank((6, 7))
                            pbf = banks[bk][:, :].bitcast(BF16)
                            for g in range(2):
                                transpose(pbf[:, g * 128:(g + 1) * 128], xbcT[:, 16 + g, csl], [B_xbc[16 + g]], bk)
                            cp_("dve", Btok[:, p, :], pbf[:, 0:256], [B_bank[bk]], [B_Btok[p]])
                            bk = nbank((0, 1, 2, 3))
                            for g in range(2):
                                mm(banks[bk][:, g * 128:(g + 1) * 128], xbcT[:, 16 + g, csl], xbcT[:, 18 + g, csl], True, True,
                                   [B_xbc[16 + g], B_xbc[18 + g]], bk)
                            tt_("dve", cbTm[:, p, :].rearrange("p (g l) -> p g l", g=2),
                                banks[bk][:, 0:256].rearrange("p (g l) -> p g l", g=2),
                                U_f[:].unsqueeze(1).to_broadcast([128, 2, 128]), ALU.mult, [B_bank[bk], B_const], [B_cbTm[p]])

                        return [lambda: xs_half(0), lambda: xs_half(1), bcb]

                    def ssd_M(c, hook=None):
                        p = c % 2
                        csl = slice(c * 128, (c + 1) * 128)
                        st = {}

                        def m0(hq):
                            rs = rseg[:, hq % 2, :].rearrange("p (h l) -> p h l", h=8)
                            tt_("pool", rs, U_f[:].unsqueeze(1).to_broadcast([128, 8, 128]),
                                hsl(c, 2, hq * 8, 8).unsqueeze(2).to_broadcast([128, 8, 128]), ALU.mult,
                                [B_const, B_dts], [B_rseg[hq % 2]])

                        def m1(hq):
                            mo = (hq % 2) * 8
                            for q in range(2):
                                bk = nbank((0, 1, 2, 3))
                                mm(banks[bk][:, :], Ls_f[:], rseg[:, hq % 2, q * 512:(q + 1) * 512], True, True,
                                   [B_rseg[hq % 2], B_const], bk)
                                act(MT[:, mo + q * 4:mo + q * 4 + 4, :].rearrange("p h l -> p (h l)"),
                                    banks[bk][:, :], AF.Exp, [B_bank[bk]], [B_MT[hq % 2]])

                        def m2(hq):
                            mo = (hq % 2) * 8
                            g = hq // 2
                            tt_("dve", MT[:, mo:mo + 8, :], MT[:, mo:mo + 8, :],
                                cbTm[:, p, g * 128:(g + 1) * 128].unsqueeze(1).to_broadcast([128, 8, 128]), ALU.mult,
                                [B_MT[hq % 2], B_cbTm[p]], [B_MT[hq % 2]])

                        def m3(hq):
                            mo = (hq % 2) * 8
                            g = hq // 2
                            ybk = nbank((0, 1, 2, 3))
                            for i in range(8):
                                h = hq * 8 + i
                                mm(banks[ybk][:, i * 64:(i + 1) * 64], MT[:, mo + i, :], xdt[:, p, h * 64:(h + 1) * 64], True, True,
                                   [B_MT[hq % 2], B_xdt[p]], ybk)
                            obk = nbank((0, 1, 2, 3))
                            mm(banks[obk][:, :], xbcT[:, 18 + g, csl], STb[:, hq * 512:(hq + 1) * 512], True, True,
                               [B_xbc[18 + g], B_STb[hq]], obk)
                            yt, Byt = ytmp[:, hq % 2, :], B_ytmp[hq % 2]
                            tt_("dve", yt.rearrange("p (h d) -> p h d", d=64),
                                banks[obk][:, :].rearrange("p (h d) -> p h d", d=64),
                                hsl(c, 5, hq * 8, 8).unsqueeze(2).to_broadcast([128, 8, 64]), ALU.mult,
                                [B_bank[obk], B_dts], [Byt])
                            tt_("dve", ytok[:, p, hq * 512:(hq + 1) * 512], banks[ybk][:, :], yt, ALU.add,
                                [B_bank[ybk], Byt], [B_ytok[p * 4 + hq]])

                        pipeline(list(range(4)), [m0, m1, m2, m3], hook=hook)

                    def ssd_S(c):
                        p = c % 2
                        for hq in range(4):
                            g = hq // 2
                            bk = nbank((0, 1, 2, 3))
                            mm(banks[bk][:, :], Btok[:, p, g * 128:(g + 1) * 128], xdtd[:, p, hq * 512:(hq + 1) * 512], True, True,
                               [B_Btok[p], B_xdtd[p]], bk)
                            stv = ST[:, hq * 512:(hq + 1) * 512]
                            tt_("dve", stv.rearrange("p (h d) -> p h d", d=64), stv.rearrange("p (h d) -> p h d", d=64),
                                hsl(c, 6, hq * 8, 8).unsqueeze(2).to_broadcast([128, 8, 64]), ALU.mult, [B_ST[hq], B_dts], [B_ST[hq]])
                            tt_("dve", stv, stv, banks[bk][:, :], ALU.add, [B_ST[hq], B_bank[bk]], [B_ST[hq]])
                            act(STb[:, hq * 512:(hq + 1) * 512], stv, AF.Copy, [B_ST[hq]], [B_STb[hq]])

                    def ssd_T_thunks(c):
                        p = c % 2
                        csl = slice(c * 128, (c + 1) * 128)
                        th = []
                        for hh in range(2):
                            stt = {}

                            def tr(hh=hh, stt=stt):
                                bk = nbank((6, 7))
                                stt["bk"] = bk
                                pbf = banks[bk][:, :].bitcast(BF16)
                                for i in range(8):
                                    cb = hh * 8 + i
                                    transpose(pbf[:, i * 128:(i + 1) * 128], ytok[:, p, cb * 128:(cb + 1) * 128], [B_ytok[p * 4 + cb // 4]], bk)
                            th.append(tr)
                            for i2 in range(4):
                                def grp(hh=hh, i2=i2, stt=stt):
                                    bk = stt["bk"]
                                    pbf = banks[bk][:, :].bitcast(BF16)
                                    pair = [(hh * 8 + i2 * 2 + d_, i2 * 2 + d_) for d_ in range(2)]
                                    for cb, i in pair:
                                        stt_("dve", tA[:, cb % 2, :], xbcT[:, cb, csl], dskip[:, cb:cb + 1], pbf[:, i * 128:(i + 1) * 128],
                                             ALU.mult, ALU.add, [B_xbc[cb], B_par, B_bank[bk]], [B_tA[cb % 2]])
                                    for cb, i in pair:
                                        stt_("dve", zs[:, cb, csl], tA[:, cb % 2, :], ssmnw[:, cb:cb + 1], zs[:, cb, csl], ALU.mult, ALU.mult,
                                             [B_tA[cb % 2], B_par, B_zs[cb]], [B_zs[cb]])
                                    for cb, i in pair:
                                        act(sq_[:, cb % 2, :], zs[:, cb, csl], AF.Square, [B_zs[cb], B_par], [B_sq[cb % 2]],
                                            scale=inv_nw[:, cb:cb + 1])
                                    for cb, i in pair:
                                        mm(banks[ssbk][:, csl], ones_b[:], sq_[:, cb % 2, :], cb == 0, cb == 15, [B_sq[cb % 2], B_const], ssbk)
                                th.append(grp)
                        return th

                    def kv_thunks(blk):
                        def kpart():
                            def ev(m, bk, blk=blk):
                                cb = blk * 2 + m
                                cp_("dve", Kc[:, cb, tt * TT:(tt + 1) * TT], banks[bk][:, :], [B_bank[bk]], [B_Kc[tt]])
                            proj_fm(w_in[l][:, C_K + blk * 256:C_K + (blk + 1) * 256], 8, 256, rhs_h, [B_hT], ev)

                        def vpart():
                            wv, wb = load_w(w_in[l][:, C_V + blk * 256:C_V + (blk + 1) * 256], 8, 256)
                            for jp in range(2):
                                bk = nbank((0, 1, 2, 3))
                                for jj in range(2):
                                    j = jp * 2 + jj
                                    for kc in range(8):
                                        mm(banks[bk][:, jj * 256:(jj + 1) * 256], hT[:, kc, j * 128:(j + 1) * 128], wv[:, kc, :],
                                           kc == 0, kc == 7, [wb, B_hT], bk)
                                cp_("dve", Vc[:, tt * 4 + jp * 2:tt * 4 + jp * 2 + 2, blk * 256:(blk + 1) * 256],
                                    banks[bk][:, :].rearrange("p (j c) -> p j c", j=2), [B_bank[bk]], [B_Vc[tt]])
                        return [kpart, vpart]

                    for th_ in ssd_F_thunks(0):
                        th_()
                    for c in range(4):
                        fill = []
                        if c >= 1:
                            fill += ssd_T_thunks(c - 1)
                        if c + 1 < 4:
                            fill += ssd_F_thunks(c + 1)
                        fill += kv_thunks(c)
                        nst = 7
                        per = (len(fill) + nst - 1) // nst

                        def hook(step, fill=fill, per=per):
                            for _ in range(per):
                                if fill:
                                    fill.pop(0)()
                        ssd_M(c, hook)
                        ssd_S(c)
                        while fill:
                            fill.pop(0)()
                    for th_ in ssd_T_thunks(3):
                        th_()
                    act(rstdb[:], banks[ssbk][:, :], AF.Ln, [B_bank[ssbk]], [B_rstdb], scale=1.0 / SSMW, bias=EPS)
                    act(rstdb[:], rstdb[:], AF.Exp, [B_rstdb], [B_rstdb], scale=-0.5)
                    tap("ygn", zs[:].rearrange("p k t -> p (k t)"), B_zs[15], [128, 16 * TT])
                    tap("rstdb", rstdb[:], B_rstdb, [128, TT])
                    fz = all_tickets(ar["bufs"])
                    for b_ in B_mg:
                        b_.r = list(fz) + list(b_.r)
                    for mb in range(8):
                        gbuf, Bg = gate_block(mb, 0)

                        def ev(m, bk, mb=mb, gbuf=gbuf, Bg=Bg):
                            tt_("dve", t1[mb % 2][:], banks[bk][:, :], rstdb[:], ALU.mult, [B_bank[bk], B_rstdb], [B_t1[mb % 2]])
                            tt_("dve", mg[:, mb, :], t1[mb % 2][:], gbuf[:], ALU.mult, [B_t1[mb % 2], Bg], [B_mg[mb]])
                        proj_fm(w_ps[l][:, mb * 128:(mb + 1) * 128], 16, 128, lambda kc: zs[:, kc, :], B_zs, ev)
                    tap("mg0", mg[:].rearrange("p k t -> p (k t)"), B_mg[7], [128, 8 * TT])

                    ar_reset()
                    ub, B_ub = ar_alloc("ub", [128, 8, TT + 16], F32, 8)
                    pa, B_pa = ar_alloc("pa", [128, 2, TT + 16], F32, 2)
                    pb_, B_pb = ar_alloc("pb", [128, 2, TT + 16], F32, 2)
                    mixed, B_mixed = ar_alloc("mixed", [128, 8, TT], BF16, 8)
                    sgp, B_sgp = ar_alloc("sgp", [128, 8, TT], BF16, 8)
                    pooled, B_pooled = ar_alloc("pooled", [128, 8, TT], BF16, 8)
                    for blk in range(4):
                        def ev(m, bk, blk=blk):
                            cb = blk * 2 + m
                            act(ub[:, cb, 16:TT + 16], banks[bk][:, :], AF.Copy, [B_bank[bk]], [B_ub[cb]])
                        proj_fm(w_in[l][:, C_PU + blk * 256:C_PU + (blk + 1) * 256], 8, 256, rhs_h, [B_hT], ev)
                    for blk in range(4):
                        def ev(m, bk, blk=blk):
                            cb = blk * 2 + m
                            act(sgp[:, cb, :], banks[bk][:, :], AF.Silu, [B_bank[bk]], [B_sgp[cb]])
                        proj_fm(w_in[l][:, C_PG + blk * 256:C_PG + (blk + 1) * 256], 8, 256, rhs_h, [B_hT], ev)
                    for cb in range(8):
                        g = cb // 2
                        if first_tile:
                            mset("dve", ub[:, cb, 0:16], 0.0, [B_ub[cb]])
                        else:
                            cp_("dve", ub[:, cb, 0:16], phalo[:, cb, :], [B_phalo[cb]], [B_ub[cb]])
                        W_ = TT + 16
                        cur, Bcur = ub[:, cb, :], B_ub[cb]
                        tmp = [(pa[:, cb % 2, :], B_pa[cb % 2]), (pb_[:, cb % 2, :], B_pb[cb % 2])]
                        for lev in range(g + 1):
                            sh = 1 << lev
                            dst, Bdst = tmp[lev % 2]
                            tt_("dve", dst[:, sh:W_], cur[:, sh:W_], cur[:, 0:W_ - sh], ALU.add, [Bcur], [Bdst])
                            if lev > 0:
                                pass
                            cur, Bcur = dst, Bdst
                        w = 2 << g
                        stt_("dve", mixed[:, cb, :], cur[:, 16:W_], 1.0 / w, ub[:, cb, 16:W_], ALU.mult, ALU.subtract,
                             [Bcur, B_ub[cb]], [B_mixed[cb]])
                        if first_tile:
                            tt_("dve", cur[:, 16:32], cur[:, 16:32], invcnt[:, g, :], ALU.mult, [Bcur, B_const], [Bcur])
                            tt_("dve", mixed[:, cb, 0:16], cur[:, 16:32], ub[:, cb, 16:32], ALU.subtract,
                                [Bcur, B_ub[cb]], [B_mixed[cb]])
                        cp_("dve", phalo[:, cb, :], ub[:, cb, TT:TT + 16], [B_ub[cb]], [B_phalo[cb]])
                    for g in range(4):
                        wv, wb = load_w(pool_w[l][g], 2, 256)
                        for m in range(2):
                            bk = nbank((0, 1, 2, 3))
                            for kc in range(2):
                                mm(banks[bk][:, :], wv[:, kc, m * 128:(m + 1) * 128], mixed[:, 2 * g + kc, :], kc == 0, kc == 1,
                                   [wb, B_mixed[2 * g + kc]], bk)
                            cb = 2 * g + m
                            stt_("dve", pooled[:, cb, :], banks[bk][:, :], pscale[:, cb:cb + 1], sgp[:, cb, :], ALU.mult, ALU.mult,
                                 [B_bank[bk], B_par, B_sgp[cb]], [B_pooled[cb]])
                    tap("pooled", pooled[:].rearrange("p k t -> p (k t)"), B_pooled[7], [128, 8 * TT])
                    for mb in range(8):
                        gbuf, Bg = gate_block(mb, 1)

                        def ev(m, bk, mb=mb, gbuf=gbuf, Bg=Bg):
                            tt_("dve", t1[mb % 2][:], banks[bk][:, :], gbuf[:], ALU.mult, [B_bank[bk], Bg], [B_t1[mb % 2]])
                            tt_("dve", mg[:, mb, :], mg[:, mb, :], t1[mb % 2][:], ALU.add, [B_t1[mb % 2], B_mg[mb]], [B_mg[mb]])
                        proj_fm(w_pp[l][:, mb * 128:(mb + 1) * 128], 8, 128, lambda kc: pooled[:, kc, :], B_pooled, ev)

                    ar_reset()
                    qT, B_qT = ar_alloc("qT", [128, 8, TT], BF16, 8)
                    sgb, B_sgb = ar_alloc("sgb", [128, 8, TT], BF16, 8)
                    sbo, B_sbo = ar_alloc("sbo", [128, 8, TT], BF16, 8)
                    NE = 4
                    e_sb, B_e = ar_alloc("e_sb", [128, NE, TT], F32, NE)
                    sp_sb, B_sp = ar_alloc("sp_sb", [128, 4, TT], BF16, 4)
                    tmp_sb, B_tmp = ar_alloc("tmp_sb", [128, NE, TT], F32, NE)
                    att_sb, B_att = ar_alloc("att_sb", [128, NE, TT], BF16, NE)
                    carry, B_carry = ar_alloc("carry", [128, 4, TT], F32, 4)
                    g2all, B_g2 = ar_alloc("g2all", [128, 8, TT], BF16, 8)
                    has_next = not (sq == NSEQ - 1 and tt == NTT - 1)
                    if has_next:
                        nxblk, B_nxblk = ar_alloc("nxblk", [128, 2, D], F32, 2)
                        nhb, B_nhb = ar_alloc("nhb", [128, 2, D], BF16, 2)
                        ntok0 = tok0 + TT
                    for blk in range(4):
                        def ev(m, bk, blk=blk):
                            cb = blk * 2 + m
                            pl.op("act", lambda e, o=qT[:, cb, :], i_=banks[bk][:, :]: e.mul(o, i_, 0.125), [B_bank[bk]], [B_qT[cb]])
                        proj_fm(w_in[l][:, C_Q + blk * 256:C_Q + (blk + 1) * 256], 8, 256, rhs_h, [B_hT], ev)
                    for blk in range(4):
                        def ev(m, bk, blk=blk):
                            cb = blk * 2 + m
                            act(sgb[:, cb, :], banks[bk][:, :], AF.Silu, [B_bank[bk]], [B_sgb[cb]])
                        proj_fm(w_in[l][:, C_SG + blk * 256:C_SG + (blk + 1) * 256], 8, 256, rhs_h, [B_hT], ev)
                    nkt = (tt + 1) * 4
                    items = []
                    for hp in range(8):
                        for a in reversed(range(nkt)):
                            items.append({"hp": hp, "a": a, "i": len(items)})
                    for it in items:
                        hp, a, i = it["hp"], it["a"], it["i"]
                        r = a - tt * 4
                        it["c0"] = max(r, 0) * 128
                        it["diag"] = r >= 0
                        it["first"] = (a == nkt - 1)
                        it["last"] = (a == 0)
                        it["zb"] = [(i % 2) * 2, (i % 2) * 2 + 1]
                        it["tb"] = [4, 5]
                        it["ob"] = 6 + (hp % 2)
                        it["e"] = [(i % 2) * 2, (i % 2) * 2 + 1]
                        it["sp"] = [(i % 2) * 2, (i % 2) * 2 + 1]
                        it["cy"] = [(hp % 2) * 2, (hp % 2) * 2 + 1]
                        it["kreads"] = [B_Kc[a // 4]]
                        it["vreads"] = [B_Vc[a // 4]]

                    def s0(it):
                        c0, hp, a = it["c0"], it["hp"], it["a"]
                        for hh in range(2):
                            zb, po = it["zb"][hh], 64 * hh
                            mm(banks[zb][:, c0:TT], Kc[po:po + 64, hp, a * 128:(a + 1) * 128], qT[po:po + 64, hp, c0:TT],
                               True, not it["diag"], it["kreads"] + [B_qT[hp]], zb)
                        if it["diag"]:
                            for hh in range(2):
                                zb = it["zb"][hh]
                                mm(banks[zb][:, c0:c0 + 128], ident_b[:], NEG_b[:], False, True, [B_const], zb, skip_group_check=True)

                    def s1(it):
                        c0 = it["c0"]
                        z0, e0, p0 = it["zb"][0], it["e"][0], it["sp"][0]
                        act(e_sb[:, e0:e0 + 2, c0:TT], ps_all[:, z0:z0 + 2, c0:TT], AF.Exp, [B_bank[z0], B_bank[z0 + 1]],
                            [B_e[e0], B_e[e0 + 1]])
                        act(sp_sb[:, p0:p0 + 2, c0:TT], e_sb[:, e0:e0 + 2, c0:TT], AF.Ln, [B_e[e0], B_e[e0 + 1]],
                            [B_sp[p0], B_sp[p0 + 1]], bias=1.0)

                    def s2(it):
                        c0 = it["c0"]
                        for hh in range(2):
                            zb, si = it["zb"][hh], it["sp"][hh]
                            mm(banks[zb][:, c0:TT], nLi_b[:], sp_sb[:, si, c0:TT], False, True, [B_sp[si], B_const], zb,
                               skip_group_check=True)
                        if not it["last"]:
                            for hh in range(2):
                                tb, si = it["tb"][hh], it["sp"][hh]
                                mm(banks[tb][:, c0:TT], nOnes_b[:], sp_sb[:, si, c0:TT], True, True, [B_sp[si], B_const], tb)

                    def s3(it):
                        c0 = it["c0"]
                        z0, e0, cy0 = it["zb"][0], it["e"][0], it["cy"][0]
                        Bcy = [B_carry[cy0], B_carry[cy0 + 1]]
                        if it["first"]:
                            mset("dve", carry[:, cy0:cy0 + 2, :], 0.0, Bcy)
                        tt_("dve", tmp_sb[:, e0:e0 + 2, c0:TT], ps_all[:, z0:z0 + 2, c0:TT], carry[:, cy0:cy0 + 2, c0:TT], ALU.add,
                            [B_bank[z0], B_bank[z0 + 1]] + Bcy, [B_tmp[e0], B_tmp[e0 + 1]])
                        if not it["last"]:
                            tt_("dve", carry[:, cy0:cy0 + 2, c0:TT], ps_all[:, 4:6, c0:TT], carry[:, cy0:cy0 + 2, c0:TT], ALU.add,
                                [B_bank[4], B_bank[5]] + Bcy, Bcy)

                    def s4(it):
                        c0 = it["c0"]
                        e0 = it["e"][0]
                        act(att_sb[:, e0:e0 + 2, c0:TT], tmp_sb[:, e0:e0 + 2, c0:TT], AF.Exp, [B_tmp[e0], B_tmp[e0 + 1]],
                            [B_att[e0], B_att[e0 + 1]])

                    def s5(it):
                        c0, ob, a, hp = it["c0"], it["ob"], it["a"], it["hp"]
                        for hh in range(2):
                            ei, po, h = it["e"][hh], 64 * hh, 2 * hp + hh
                            mm(banks[ob][po:po + 64, c0:TT], Vc[:, a, h * 64:(h + 1) * 64], att_sb[:, ei, c0:TT],
                               it["first"], it["last"], it["vreads"] + [B_att[ei]], ob, skip_group_check=True)
                        if it["last"]:
                            tt_("dve", sbo[:, hp, :], banks[ob][:, :], sgb[:, hp, :], ALU.mult, [B_bank[ob], B_sgb[hp]], [B_sbo[hp]])

                    for mb in range(8):
                        c0g = C_MG + 2 * D + mb * 128

                        def evg(m, bk, mb=mb):
                            act(g2all[:, mb, :], banks[bk][:, :], AF.Sigmoid, [B_bank[bk]], [B_g2[mb]])
                        proj_fm(w_in[l][:, c0g:c0g + 128], 8, 128, rhs_h, [B_hT], evg)
                    hook = None
                    if has_next:
                        nsteps = len(items) + 4
                        at = {max(2, (nsteps * (jn + 1)) // 6): jn for jn in range(4)}

                        def hook(step):
                            if step in at:
                                norm_block(ntok0, at[step], nxblk, B_nxblk, nhb, B_nhb, tpool=(4, 5))
                    pipeline(items, [s0, s1, lambda it: (s2(it), s3(it)), s4, s5], order=[2, 4, 3, 1, 0], hook=hook)
                    if has_next:
                        prefetched["v"] = True
                    tap("sbo", sbo[:].rearrange("p k t -> p (k t)"), B_sbo[7], [128, 8 * TT])
                    for mb in range(8):
                        gbuf, Bg = g2all[:, mb, :], B_g2[mb]

                        def ev(m, bk, mb=mb, gbuf=gbuf, Bg=Bg):
                            tt_("dve", t1[mb % 2][:], banks[bk][:, :], gbuf, ALU.mult, [B_bank[bk], Bg], [B_t1[mb % 2]])
                            tt_("dve", mg[:, mb, :], mg[:, mb, :], t1[mb % 2][:], ALU.add, [B_t1[mb % 2], B_mg[mb]], [B_mg[mb]])
                        proj_fm(w_pb[l][:, mb * 128:(mb + 1) * 128], 8, 128, lambda kc: sbo[:, kc, :], B_sbo, ev)

                    ar_reset()
                    mgb, B_mgb = ar_alloc("mgb", [128, 8, TT], BF16, 8)
                    xo, B_xo = ar_alloc("xo", [128, 4, D], F32, 4)
                    if last and final:
                        junk, B_junk = ar_alloc("junk", [128, D], BF16, 1)
                        fnw, B_fnw = ar_alloc("fnw", [128, D], F32, 1)
                        pl.dma("sp", fnw, p_fnw, [], [B_fnw[0]], B_fnw[0])
                    for mb in range(8):
                        cp_("dve", mgb[:, mb, :], mg[:, mb, :], [B_mg[mb]], [B_mgb[mb]])
                    for j in range(4):
                        r0 = tok0 + j * 128
                        pl.dma("sp", xo[:, j, :], src_x[r0:r0 + 128, :], [xbuf(src_id, r0)], [B_xo[j]], B_xo[j])
                    for half in range(2):
                        bks = [nbank((0, 1, 2, 3)) for j in range(4)]
                        for kh in range(2):
                            wv, wb = load_w(w_out[l][kh * 512:(kh + 1) * 512, half * 512:(half + 1) * 512], 4, 512)
                            for j in range(4):
                                for k4 in range(4):
                                    kc = kh * 4 + k4
                                    mm(banks[bks[j]][:, :], mgb[:, kc, j * 128:(j + 1) * 128], wv[:, k4, :], kc == 0, kc == 7,
                                       [wb, B_mgb[kc]], bks[j])
                        for j in range(4):
                            xv = xo[:, j, half * 512:(half + 1) * 512]
                            tt_("dve", xv, banks[bks[j]][:, :], xv, ALU.add, [B_bank[bks[j]], B_xo[j]], [B_xo[j]])
                    for j in range(4):
                        r0 = tok0 + j * 128
                        xov, Bxo = xo[:, j, :], B_xo[j]
                        if last and final:
                            ssj, Bss = sst[:, (j % 2) * 4:(j % 2) * 4 + 1], B_sst[j % 2]
                            rsj = sst[:, (j % 2) * 4 + 1:(j % 2) * 4 + 2]
                            mset("dve", ssj, 0.0, [Bss])
                            act(junk, xov, AF.Square, [Bxo, Bss], [B_junk[0], Bss], accum_out=ssj)
                            act(rsj, ssj, AF.Ln, [Bss], [Bss], scale=1.0 / D, bias=EPS)
                            act(rsj, rsj, AF.Exp, [Bss], [Bss], scale=-0.5)
                            stt_("dve", xov, xov, rsj, fnw, ALU.mult, ALU.mult, [Bxo, Bss, B_fnw[0]], [Bxo])
                        pl.dma("sp", dst_x[r0:r0 + 128, :], xov, [Bxo], [xbuf(dst_id, r0)], Bxo)
                        if last:
                            outs_final.append(xbuf(dst_id, r0))
        pl.final_wait("sp", outs_final + [v[1] for v in tapd.values()])
        pl.emit()
    return nc, pl


def _prep_inputs(inp, NL, l0=0):
    f = lambda a: np.ascontiguousarray(np.asarray(a, dtype=np.float32))
    sl = slice(l0, l0 + NL)
    m = {}
    m["w_in"] = f(inp["w_in"][sl])
    m["w_ps"] = f(inp["w_proj_ssm"][sl])
    m["w_pp"] = f(inp["w_proj_pool"][sl])
    m["w_pb"] = f(inp["w_proj_sb"][sl])
    m["w_out"] = f(inp["w_out"][sl])
    m["pool_w"] = f(inp["pool_w"][sl])
    rep = lambda a: f(np.broadcast_to(np.asarray(a)[:, None, :], (NL, 128, np.asarray(a).shape[-1])))
    col = lambda a, n: f(np.asarray(a).reshape(NL, n, 128).transpose(0, 2, 1))
    m["p_normw"] = rep(inp["norm_w"][sl])
    cw = np.asarray(inp["conv_w"][sl])
    m["p_convw"] = f(cw.transpose(0, 2, 1).reshape(NL, 20, 128, 4).transpose(0, 2, 1, 3).reshape(NL, 128, 80))
    m["p_convb"] = col(inp["conv_b"][sl], 20)
    m["p_dtb"] = rep(inp["dt_bias"][sl])
    m["p_alog"] = rep(inp["a_log"][sl])
    m["p_dskip"] = col(np.repeat(np.asarray(inp["d_skip"][sl]), 64, axis=1), 16)
    m["p_ssmnw"] = col(inp["ssm_norm_w"][sl], 16)
    m["p_pscale"] = col(inp["pool_scale"][sl], 8)
    m["p_fnw"] = f(np.broadcast_to(np.asarray(inp["final_norm_w"])[None, :], (128, D)))
    m.update(host_consts())
    return m


_CACHE = {}


def kernel(**inputs):
    x = np.asarray(inputs["x"], dtype=np.float32)
    Bsz, S, _ = x.shape
    NL = inputs["w_in"].shape[0]
    NCORE = 8
    NSEQ = Bsz // NCORE
    key = (S, NSEQ, NL)
    if key not in _CACHE:
        _CACHE[key] = build(S, NSEQ, NL, True)[0]
    nc = _CACHE[key]
    shared = _prep_inputs(inputs, NL)
    in_maps = []
    for c in range(NCORE):
        mp = dict(shared)
        mp["x"] = np.ascontiguousarray(x[c * NSEQ:(c + 1) * NSEQ].reshape(NSEQ * S, D))
        in_maps.append(mp)
    res = run_bass_kernel_spmd(nc, in_maps, core_ids=list(range(NCORE)))
    out = np.concatenate([r["out"].reshape(NSEQ, S, D) for r in res.results], axis=0)
    return out.astype(np.float32)
```

```python
import contextlib
import numpy as np
import concourse.bass as bass
import concourse.mybir as mybir
from concourse.bass_utils import run_bass_kernel_spmd

F32 = mybir.dt.float32
BF16 = mybir.dt.bfloat16
AF = mybir.ActivationFunctionType
ALU = mybir.AluOpType

D = 1024
SSMW = 2048
NH = 32
CONVC = 2560
INC = 13856
C_Z, C_XBC, C_DT, C_PU, C_PG, C_Q, C_K, C_V, C_SG, C_MG = 0, 2048, 4608, 4640, 5664, 6688, 7712, 8736, 9760, 10784
EPS = 1e-6
TT = 512
SAME_ENGINE_SYNC = True
NO_SELF_SYNC = ("pe",)


class Buf:
    __slots__ = ("w", "r", "sem", "name")

    def __init__(self, name=""):
        self.w = None
        self.r = []
        self.sem = None
        self.name = name


class SemH:
    def __init__(self, h, name):
        self.h = h
        self.name = name
        self.total = 0


class Plan:
    ENGS = ("pe", "dve", "act", "pool", "sp")

    def __init__(self, nc, stack):
        self.nc = nc
        self.stack = stack
        self.streams = {e: [] for e in self.ENGS}
        self.cnt = {e: 0 for e in self.ENGS}
        self.waited = {e: {} for e in self.ENGS}
        self.nsem = 0
        self.esem = {e: self.new_sem("e_" + e) for e in self.ENGS}
        self.ninst = 0
        self.semcache = {}

    def new_sem(self, name):
        self.nsem += 1
        return SemH(self.stack.enter_context(self.nc.semaphore(name)), name)

    def _wait(self, eng, tickets):
        need = {}
        for (sem, val, isdma) in tickets:
            if isdma:
                val = sem.total
            elif sem is self.esem[eng] and (eng in NO_SELF_SYNC or not SAME_ENGINE_SYNC):
                continue
            if need.get(sem, 0) < val:
                need[sem] = val
        for sem, val in need.items():
            if self.waited[eng].get(sem, 0) < val:
                self.streams[eng].append(("w", sem, val))
                self.waited[eng][sem] = val

    def _deps(self, reads, writes):
        t = []
        for b in reads:
            if b.w is not None:
                t.append(b.w)
        for b in writes:
            if b.w is not None:
                t.append(b.w)
            t.extend(b.r)
        return t

    def op(self, eng, fn, reads=(), writes=()):
        ex = [b for b in reads if b.name.startswith("bank")]
        if ex:
            reads = [b for b in reads if not b.name.startswith("bank")]
            writes = list(writes) + ex
        self._wait(eng, self._deps(reads, writes))
        sem = self.esem[eng]
        self.cnt[eng] += 1
        sem.total = self.cnt[eng]
        tk = (sem, self.cnt[eng], False)
        self.streams[eng].append(("i", fn, sem, 1))
        for b in reads:
            b.r.append(tk)
        for b in writes:
            b.w = tk
            b.r = []
        self.ninst += 1

    def dma(self, eng, out_ap, in_ap, reads, writes, owner):
        self._wait(eng, self._deps(reads, writes))
        if owner.sem is None:
            if owner.name not in self.semcache:
                self.semcache[owner.name] = self.new_sem("d_" + owner.name)
            owner.sem = self.semcache[owner.name]
        sem = owner.sem
        sem.total += 16
        tk = (sem, sem.total, True)
        self.streams[eng].append(("i", lambda e: e.dma_start(out=out_ap, in_=in_ap), sem, 16))
        for b in reads:
            b.r.append(tk)
        for b in writes:
            b.w = tk
            b.r = []
        self.ninst += 1

    def final_wait(self, eng, bufs):
        t = []
        for b in bufs:
            if b.w is not None:
                t.append(b.w)
            t.extend(b.r)
        self._wait(eng, t)

    def emit(self):
        nc = self.nc
        streams = self.streams

        def replay(name, e):
            for rec in streams[name]:
                if rec[0] == "w":
                    e.wait_ge(rec[1].h, rec[2])
                else:
                    rec[1](e).then_inc(rec[2].h, rec[3])

        with nc.Block() as block:
            @block.tensor
            def _(e):
                replay("pe", e)

            @block.vector
            def _(e):
                replay("dve", e)

            @block.scalar
            def _(e):
                replay("act", e)

            @block.gpsimd
            def _(e):
                replay("pool", e)

            @block.sync
            def _(e):
                replay("sp", e)


def pipeline(items, stages, order=None, hook=None):
    n, d = len(items), len(stages)
    order = list(reversed(range(d))) if order is None else order
    for step in range(n + d - 1):
        if hook is not None:
            hook(step)
        for si in order:
            st = stages[si]
            k = step - si
            if 0 <= k < n:
                st(items[k])


def host_consts():
    i = np.arange(128)
    c = {}
    c["c_ident"] = np.eye(128, dtype=np.float32)
    c["c_U"] = (i[:, None] <= i[None, :]).astype(np.float32)
    c["c_Ls"] = (i[:, None] > i[None, :]).astype(np.float32)
    c["c_ones"] = np.ones((128, 128), np.float32)
    c["c_nLi"] = -(i[:, None] >= i[None, :]).astype(np.float32)
    c["c_nOnes"] = -np.ones((128, 128), np.float32)
    c["c_NEG"] = -30000.0 * (i[:, None] >= i[None, :]).astype(np.float32)
    inv = np.zeros((128, 4, 16), np.float32)
    for g, w in enumerate((2, 4, 8, 16)):
        inv[:, g, :] = 1.0 / np.minimum(np.arange(16) + 1, w)
    c["c_invcnt"] = inv.reshape(128, 64)
    return c


def build(S=2048, NSEQ=2, NL=4, final=True, taps=()):
    nc = bass.Bass("TRN2", target_bir_lowering=False)
    NT = NSEQ * S
    NTT = S // TT
    stack = contextlib.ExitStack()
    with stack:
        pl = Plan(nc, stack)

        def dram(name, shape, dt=F32, kind="ExternalInput"):
            return nc.dram_tensor(name, list(shape), dt, kind=kind).ap()

        x_in = dram("x", [NT, D])
        y_out = dram("out", [NT, D], kind="ExternalOutput")
        w_in = dram("w_in", [NL, D, INC])
        w_ps = dram("w_ps", [NL, SSMW, D])
        w_pp = dram("w_pp", [NL, D, D])
        w_pb = dram("w_pb", [NL, D, D])
        w_out = dram("w_out", [NL, D, D])
        pool_w = dram("pool_w", [NL, 4, 256, 256])
        p_normw = dram("p_normw", [NL, 128, D])
        p_convw = dram("p_convw", [NL, 128, 80])
        p_convb = dram("p_convb", [NL, 128, 20])
        p_dtb = dram("p_dtb", [NL, 128, 32])
        p_alog = dram("p_alog", [NL, 128, 32])
        p_dskip = dram("p_dskip", [NL, 128, 16])
        p_ssmnw = dram("p_ssmnw", [NL, 128, 16])
        p_pscale = dram("p_pscale", [NL, 128, 8])
        p_fnw = dram("p_fnw", [128, D])
        cdr = {k: dram(k, v.shape) for k, v in host_consts().items()}
        xscr = [dram("xscr%d" % i, [NT, D], kind="Internal") for i in range(2)] if NL > 1 else []
        tapd = {}

        def sb(name, shape, dt):
            return stack.enter_context(nc.sbuf_tensor(name, list(shape), dt))

        cst_f = {k: sb("f" + k, [128, 128], F32) for k in ("c_U", "c_Ls", "c_ones")}
        cst_b = {k: sb("b" + k, [128, 128], BF16) for k in ("c_ident", "c_ones", "c_nLi", "c_nOnes", "c_NEG")}
        invcnt = sb("invcnt", [128, 4, 16], F32)
        B_const = Buf("const")
        for k, t in cst_f.items():
            pl.dma("sp", t[:], cdr[k], [], [B_const], B_const)
        for k, t in cst_b.items():
            pl.dma("pool", t[:], cdr[k], [], [B_const], B_const)
        pl.dma("sp", invcnt[:].rearrange("p a b -> p (a b)"), cdr["c_invcnt"], [], [B_const], B_const)
        U_f, Ls_f, ones_f = cst_f["c_U"], cst_f["c_Ls"], cst_f["c_ones"]
        ident_b, ones_b, nLi_b, nOnes_b, NEG_b = (cst_b[k] for k in ("c_ident", "c_ones", "c_nLi", "c_nOnes", "c_NEG"))

        normw = sb("normw", [128, D], F32)
        convw = sb("convw", [128, 20, 4], F32)
        convb = sb("convb", [128, 20], F32)
        dtb = sb("dtb", [128, 32], F32)
        a_neg = sb("a_neg", [128, 32], F32)
        dskip = sb("dskip", [128, 16], F32)
        ssmnw = sb("ssmnw", [128, 16], F32)
        inv_nw = sb("inv_nw", [128, 16], F32)
        pscale = sb("pscale", [128, 8], F32)
        B_par = Buf("par")

        Kc = sb("Kc", [128, 8, S], BF16)
        Vc = sb("Vc", [128, S // 128, D], BF16)
        ST = sb("ST", [128, 2048], F32)
        STb = sb("STb", [128, 2048], BF16)
        chalo = sb("chalo", [128, 20, 3], F32)
        B_Kc = [Buf("Kc%d" % i) for i in range(NTT)]
        B_Vc = [Buf("Vc%d" % i) for i in range(NTT)]
        phalo = sb("phalo", [128, 8, 16], F32)
        B_ST = [Buf("ST%d" % i) for i in range(4)]
        B_STb = [Buf("STb%d" % i) for i in range(4)]
        B_chalo = [Buf("chalo%d" % i) for i in range(20)]
        B_phalo = [Buf("phalo%d" % i) for i in range(8)]

        sst = sb("sst", [128, 8], F32)
        B_sst = [Buf("sst%d" % i) for i in range(2)]
        hT = sb("hT", [128, 8, TT], BF16)
        B_hT = Buf("hT")
        NSLOT = 3
        wsl = [sb("wsl%d" % i, [128, 2048], BF16) for i in range(NSLOT)]
        B_wsl = [Buf("wsl%d" % i) for i in range(NSLOT)]
        B_mg = [Buf("mg%d" % i) for i in range(8)]
        gate = [sb("gate%d" % i, [128, TT], BF16) for i in range(2)]
        B_gate = [Buf("gate%d" % i) for i in range(2)]
        t1 = [sb("t1_%d" % i, [128, TT], F32) for i in range(2)]
        B_t1 = [Buf("t1_%d" % i) for i in range(2)]
        ARENA_F = 24064
        MG_OFF = ARENA_F - 8 * TT
        arena = sb("arena", [128, ARENA_F], F32)
        mg = arena[:, MG_OFF:ARENA_F].rearrange("p (a b) -> p a b", a=8)
        ar = {"off": 0, "bufs": [], "fence": [], "limit": MG_OFF}

        def all_tickets(bufs):
            f = []
            for b in bufs:
                if b.w is not None:
                    f.append(b.w)
                f.extend(b.r)
            best = {}
            for (s_, v, d_) in f:
                if s_ not in best or best[s_][1] < v:
                    best[s_] = (s_, v, d_)
            return list(best.values())

        class _F:
            pass

        def ar_reset(limit=None):
            fb = _F()
            fb.w = None
            fb.r = list(ar["fence"])
            ar["fence"] = all_tickets([fb] + ar["bufs"] + B_mg)
            ar["bufs"] = []
            ar["off"] = 0
            ar["limit"] = MG_OFF if limit is None else limit

        def ar_alloc(name, shape, dt, nbuf=1):
            n = int(np.prod(shape[1:]))
            words = n if dt == F32 else (n + 1) // 2
            o = ar["off"]
            assert o + words <= ar["limit"], ("arena overflow", name, o, words, ar["limit"])
            ar["off"] = o + words
            ap = arena[:, o:o + words]
            if dt != F32:
                ap = ap.bitcast(BF16)
                if n % 2:
                    ap = ap[:, 0:n]
            if len(shape) == 3:
                ap = ap.rearrange("p (a b) -> p a b", a=shape[1])
            elif len(shape) == 4:
                ap = ap.rearrange("p (a b c) -> p a b c", a=shape[1], b=shape[2])
            bufs = []
            for i in range(nbuf):
                b = Buf(name + str(i))
                b.r = list(ar["fence"])
                ar["bufs"].append(b)
                bufs.append(b)
            return ap, bufs

        ps_all = stack.enter_context(nc.psum_tensor("ps_all", [128, 8, 512], F32))
        banks = [ps_all[:, i, :] for i in range(8)]
        B_bank = [Buf("bank%d" % i) for i in range(8)]
        rr = {"i": 0}

        def nbank(pool=(0, 1, 2, 3, 4, 5, 6, 7)):
            i = pool[rr["i"] % len(pool)]
            rr["i"] += 1
            return i

        slot_i = {"i": 0}

        def load_w(src, KC, C):
            si = slot_i["i"] % NSLOT
            slot_i["i"] += 1
            view = wsl[si][:, 0:KC * C].rearrange("p (k c) -> p k c", k=KC)
            pl.dma("pool", view, src.rearrange("(k p) c -> p k c", p=128), [], [B_wsl[si]], B_wsl[si])
            return view, B_wsl[si]

        def mm(out_ap, lhsT, rhs, start, stop, reads, bank, **kw):
            pl.op("pe", lambda e: e.matmul(out_ap, lhsT, rhs, start=start, stop=stop, **kw), reads, [B_bank[bank]])

        def act(out_ap, in_ap, func, reads, writes, **kw):
            pl.op("act", lambda e: e.activation(out=out_ap, in_=in_ap, func=func, **kw), reads, writes)

        def tt_(eng, out_ap, in0, in1, op, reads, writes):
            pl.op(eng, lambda e: e.tensor_tensor(out=out_ap, in0=in0, in1=in1, op=op), reads, writes)

        def ts_(eng, out_ap, in0, s1, s2, op0, op1, reads, writes):
            if s2 is None:
                pl.op(eng, lambda e: e.tensor_scalar(out=out_ap, in0=in0, scalar1=s1, scalar2=None, op0=op0), reads, writes)
            else:
                pl.op(eng, lambda e: e.tensor_scalar(out=out_ap, in0=in0, scalar1=s1, scalar2=s2, op0=op0, op1=op1), reads, writes)

        def stt_(eng, out_ap, in0, scalar, in1, op0, op1, reads, writes):
            pl.op(eng, lambda e: e.scalar_tensor_tensor(out=out_ap, in0=in0, scalar=scalar, in1=in1, op0=op0, op1=op1), reads, writes)

        def cp_(eng, out_ap, in_ap, reads, writes):
            pl.op(eng, lambda e: e.tensor_copy(out=out_ap, in_=in_ap), reads, writes)

        def mset(eng, ap, val, writes):
            pl.op(eng, lambda e: e.memset(ap, val), [], writes)

        def transpose(out_ap, in_ap, reads, bank):
            pl.op("pe", lambda e: e.transpose(out_ap, in_ap, ident_b[:]), list(reads) + [B_const], [B_bank[bank]])

        def proj_fm(wsrc, KC, ncols, rhs_fn, rhs_reads, evac, mm_pool=(0, 1, 2, 3), joint=False):
            wv, wb = load_w(wsrc, KC, ncols)
            done = []
            for m in range(ncols // 128):
                bk = nbank(mm_pool)
                for kc in range(KC):
                    mm(banks[bk][:, :], wv[:, kc, m * 128:(m + 1) * 128], rhs_fn(kc), kc == 0, kc == KC - 1,
                       [wb] + list(rhs_reads), bk)
                if joint:
                    done.append((m, bk))
                else:
                    evac(m, bk)
            if joint:
                evac(done)

        def tap(name, ap_sb, buf, shape):
            if name in taps:
                if name not in tapd:
                    tapd[name] = (dram("tap_" + name, shape, ap_sb.dtype, kind="ExternalOutput"), Buf("tap_" + name))
                pl.dma("sp", tapd[name][0], ap_sb, [buf], [tapd[name][1]], buf)

        B_xd = {}

        def xbuf(which, r0):
            return B_xd.setdefault((which, r0), Buf("xd"))

        outs_final = []

        for l in range(NL):
            src_x, src_id = (x_in, "in") if l == 0 else (xscr[(l - 1) % 2], "s%d" % ((l - 1) % 2))
            last = (l == NL - 1)
            dst_x, dst_id = (y_out, "out") if last else (xscr[l % 2], "s%d" % (l % 2))
            for (t, srcp) in ((normw, p_normw[l]), (convb, p_convb[l]), (dtb, p_dtb[l]), (a_neg, p_alog[l]),
                              (dskip, p_dskip[l]), (ssmnw, p_ssmnw[l]), (pscale, p_pscale[l])):
                pl.dma("sp", t[:], srcp, [], [B_par], B_par)
            pl.dma("sp", convw[:].rearrange("p a b -> p (a b)"), p_convw[l], [], [B_par], B_par)
            act(a_neg[:], a_neg[:], AF.Exp, [B_par], [B_par])
            ts_("dve", a_neg[:], a_neg[:], -1.0, None, ALU.mult, None, [B_par], [B_par])
            pl.op("dve", lambda e: e.reciprocal(out=inv_nw[:], in_=ssmnw[:]), [B_par], [B_par])

            prefetched = {"v": False}
            for sq in range(NSEQ):
                mset("dve", ST[:], 0.0, B_ST)
                mset("dve", STb[:], 0.0, B_STb)
                mset("dve", chalo[:], 0.0, B_chalo)
                for tt in range(NTT):
                    tok0 = sq * S + tt * TT
                    first_tile = (tt == 0)
                    def norm_block(tokb, j, xblk_a, B_xblk, hb_a, B_hb, tpool=(6, 7)):
                        r0 = tokb + j * 128
                        xb, Bx = xblk_a[:, j % 2, :], B_xblk[j % 2]
                        hbj, Bh = hb_a[:, j % 2, :], B_hb[j % 2]
                        ssj, Bss = sst[:, (j % 2) * 4:(j % 2) * 4 + 1], B_sst[j % 2]
                        rsj = sst[:, (j % 2) * 4 + 1:(j % 2) * 4 + 2]
                        pl.dma("sp", xb, src_x[r0:r0 + 128, :], [xbuf(src_id, r0)], [Bx], Bx)
                        mset("dve", ssj, 0.0, [Bss])
                        act(hbj, xb, AF.Square, [Bx, Bss], [Bh, Bss], accum_out=ssj)
                        act(rsj, ssj, AF.Ln, [Bss], [Bss], scale=1.0 / D, bias=EPS)
                        act(rsj, rsj, AF.Exp, [Bss], [Bss], scale=-0.5)
                        stt_("dve", hbj, xb, rsj, normw[:], ALU.mult, ALU.mult, [Bx, Bss, B_par], [Bh])
                        bk = nbank(tpool)
                        pbf = banks[bk][:, :].bitcast(BF16)
                        for kc in range(8):
                            transpose(pbf[:, kc * 128:(kc + 1) * 128], hbj[:, kc * 128:(kc + 1) * 128], [Bh], bk)
                        cp_("dve", hT[:, :, j * 128:(j + 1) * 128], pbf.rearrange("p (k t) -> p k t", k=8),
                            [B_bank[bk]], [B_hT])

                    if not prefetched["v"]:
                        ar_reset()
                        xblk_a, B_xblk = ar_alloc("xblk", [128, 2, D], F32, 2)
                        hb_a, B_hb = ar_alloc("hb", [128, 2, D], BF16, 2)
                        for j in range(4):
                            norm_block(tok0, j, xblk_a, B_xblk, hb_a, B_hb)
                    prefetched["v"] = False
                    tap("hT", hT[:].rearrange("p k t -> p (k t)"), B_hT, [128, 8 * TT])

                    rhs_h = lambda kc: hT[:, kc, :]

                    def gate_block(mb, gi):
                        if mb % 2 == 0:
                            c0 = C_MG + gi * D + mb * 128

                            def ev(m, bk):
                                act(gate[m][:], banks[bk][:, :], AF.Sigmoid, [B_bank[bk]], [B_gate[m]])
                            proj_fm(w_in[l][:, c0:c0 + 256], 8, 256, rhs_h, [B_hT], ev)
                        return gate[mb % 2], B_gate[mb % 2]

                    ar_reset(ARENA_F)
                    zs, B_zs = ar_alloc("zs", [128, 16, TT], BF16, 16)
                    rstdb, B_rstdb = ar_alloc("rstdb", [128, TT], F32, 1)
                    xbcT, B_xbc = ar_alloc("xbcT", [128, 20, TT], BF16, 20)
                    upre, B_upre = ar_alloc("upre", [128, 2, TT + 4], F32, 2)
                    cacc, B_cacc = ar_alloc("cacc", [128, 2, TT], F32, 2)
                    dts, B_dts = ar_alloc("dts", [128, 8, 128], F32, 1)
                    B_dts = B_dts[0]
                    xdt, B_xdt = ar_alloc("xdt", [128, 2, 2048], BF16, 2)
                    xdtd, B_xdtd = ar_alloc("xdtd", [128, 2, 2048], BF16, 2)
                    Btok, B_Btok = ar_alloc("Btok", [128, 2, 256], BF16, 2)
                    cbTm, B_cbTm = ar_alloc("cbTm", [128, 2, 256], BF16, 2)
                    rseg, B_rseg = ar_alloc("rseg", [128, 2, 1024], F32, 2)
                    MT, B_MT = ar_alloc("MT", [128, 16, 128], BF16, 2)
                    ytmp, B_ytmp = ar_alloc("ytmp", [128, 2, 512], F32, 2)
                    ytok, B_ytok = ar_alloc("ytok", [128, 2, 2048], BF16, 8)
                    tA, B_tA = ar_alloc("tA", [128, 2, 128], F32, 2)
                    sq_, B_sq = ar_alloc("sq", [128, 2, 128], BF16, 2)
                    B_rstdb = B_rstdb[0]

                    wv, wb = load_w(w_in[l][:, C_DT:C_DT + 32], 8, 32)
                    bk = nbank((0, 1, 2, 3))
                    for j in range(4):
                        for kc in range(8):
                            mm(banks[bk][:, j * 32:(j + 1) * 32], hT[:, kc, j * 128:(j + 1) * 128], wv[:, kc, :],
                               kc == 0, kc == 7, [wb, B_hT], bk)
                    d3 = lambda i: dts[:, i, :].rearrange("p (j h) -> p j h", j=4)
                    bc4 = lambda t: t[:].unsqueeze(1).to_broadcast([128, 4, 32])
                    tt_("dve", d3(0), banks[bk][:, 0:128].rearrange("p (j h) -> p j h", j=4), bc4(dtb), ALU.add,
                        [B_bank[bk], B_par], [B_dts])
                    act(dts[:, 0, :], dts[:, 0, :], AF.Exp, [B_dts], [B_dts])
                    act(dts[:, 1, :], dts[:, 0, :], AF.Ln, [B_dts], [B_dts], bias=1.0)
                    tt_("dve", d3(2), d3(1), bc4(a_neg), ALU.mult, [B_dts, B_par], [B_dts])
                    bk = nbank((0, 1, 2, 3))
                    mm(banks[bk][:, 0:128], U_f[:], dts[:, 2, :], True, True, [B_dts, B_const], bk)
                    mm(banks[bk][:, 128:256], ones_f[:], dts[:, 2, :], True, True, [B_dts, B_const], bk)
                    cp_("dve", dts[:, 3, :], banks[bk][:, 0:128], [B_bank[bk]], [B_dts])
                    tt_("dve", dts[:, 4, :], banks[bk][:, 128:256], dts[:, 3, :], ALU.subtract, [B_bank[bk], B_dts], [B_dts])
                    act(dts[:, 4, :], dts[:, 4, :], AF.Exp, [B_dts], [B_dts])
                    act(dts[:, 5, :], dts[:, 3, :], AF.Exp, [B_dts], [B_dts])
                    act(dts[:, 6, :], banks[bk][:, 128:256], AF.Exp, [B_bank[bk]], [B_dts])
                    tt_("dve", dts[:, 7, :], dts[:, 1, :], dts[:, 4, :], ALU.mult, [B_dts], [B_dts])
                    tap("dts", dts[:].rearrange("p k t -> p (k t)"), B_dts, [128, 1024])
                    def z_blk(blk):
                        def ev(m, bk, blk=blk):
                            cb = blk * 2 + m
                            act(zs[:, cb, :], banks[bk][:, :], AF.Silu, [B_bank[bk]], [B_zs[cb]])
                        proj_fm(w_in[l][:, C_Z + blk * 256:C_Z + (blk + 1) * 256], 8, 256, rhs_h, [B_hT], ev)
                    def x_blk(blk):
                        def ev(done, blk=blk):
                            cbs = [(blk * 2 + m, bk) for (m, bk) in done]
                            for cb, bk in cbs:
                                u, Bu = upre[:, cb % 2, :], B_upre[cb % 2]
                                ca, Bca = cacc[:, cb % 2, :], B_cacc[cb % 2]
                                act(u[:, 4:TT + 4], banks[bk][:, :], AF.Copy, [B_bank[bk]], [Bu])
                                act(ca, banks[bk][:, :], AF.Identity, [B_bank[bk], B_par], [Bca],
                                    scale=convw[:, cb, 3:4], bias=convb[:, cb:cb + 1])
                                cp_("dve", u[:, 1:4], chalo[:, cb, :], [B_chalo[cb]], [Bu])
                            for k in (2, 1, 0):
                                for cb, bk in cbs:
                                    u, Bu = upre[:, cb % 2, :], B_upre[cb % 2]
                                    ca, Bca = cacc[:, cb % 2, :], B_cacc[cb % 2]
                                    stt_("dve", ca, u[:, 1 + k:1 + k + TT], convw[:, cb, k:k + 1], ca, ALU.mult, ALU.add,
                                         [Bu, B_par, Bca], [Bca])
                            for cb, bk in cbs:
                                u, Bu = upre[:, cb % 2, :], B_upre[cb % 2]
                                ca, Bca = cacc[:, cb % 2, :], B_cacc[cb % 2]
                                cp_("dve", chalo[:, cb, :], u[:, TT + 1:TT + 4], [Bu], [B_chalo[cb]])
                                act(xbcT[:, cb, :], ca, AF.Silu, [Bca], [B_xbc[cb]])
                        proj_fm(w_in[l][:, C_XBC + blk * 256:C_XBC + (blk + 1) * 256], 8, 256, rhs_h, [B_hT], ev, joint=True)
                    for blk in range(10):
                        x_blk(blk)
                        if blk < 8:
                            z_blk(blk)
                    tap("zs", zs[:].rearrange("p k t -> p (k t)"), B_zs[15], [128, 16 * TT])
                    tap("xbcT", xbcT[:].rearrange("p k t -> p (k t)"), B_xbc[19], [128, 20 * TT])
                    ssbk = nbank((4, 5))

                    def hsl(c, i, h0, n):
                        return dts[:, i, c * 32 + h0:c * 32 + h0 + n]

                    def ssd_F_thunks(c):
                        p = c % 2
                        csl = slice(c * 128, (c + 1) * 128)

                        def xs_half(hh):
                            bk = nbank((6, 7))
                            pbf = banks[bk][:, :].bitcast(BF16)
                            for i in range(8):
                                cb = hh * 8 + i
                                transpose(pbf[:, i * 128:(i + 1) * 128], xbcT[:, cb, csl], [B_xbc[cb]], bk)
                            pv = pbf.rearrange("p (h d) -> p h d", d=64)
                            tt_("dve", xdt[:, p, hh * 1024:(hh + 1) * 1024].rearrange("p (h d) -> p h d", d=64), pv,
                                hsl(c, 1, hh * 16, 16).unsqueeze(2).to_broadcast([128, 16, 64]), ALU.mult,
                                [B_bank[bk], B_dts], [B_xdt[p]])
                            tt_("dve", xdtd[:, p, hh * 1024:(hh + 1) * 1024].rearrange("p (h d) -> p h d", d=64), pv,
                                hsl(c, 7, hh * 16, 16).unsqueeze(2).to_broadcast([128, 16, 64]), ALU.mult,
                                [B_bank[bk], B_dts], [B_xdtd[p]])

                        def bcb():
                            bk = nbank((6, 7))
                            pbf = banks[bk][:, :].bitcast(BF16)
                            for g in range(2):
                                transpose(pbf[:, g * 128:(g + 1) * 128], xbcT[:, 16 + g, csl], [B_xbc[16 + g]], bk)
                            cp_("dve", Btok[:, p, :], pbf[:, 0:256], [B_bank[bk]], [B_Btok[p]])
                            bk = nbank((0, 1, 2, 3))
                            for g in range(2):
                                mm(banks[bk][:, g * 128:(g + 1) * 128], xbcT[:, 16 + g, csl], xbcT[:, 18 + g, csl], True, True,
                                   [B_xbc[16 + g], B_xbc[18 + g]], bk)
                            tt_("dve", cbTm[:, p, :].rearrange("p (g l) -> p g l", g=2),
                                banks[bk][:, 0:256].rearrange("p (g l) -> p g l", g=2),
                                U_f[:].unsqueeze(1).to_broadcast([128, 2, 128]), ALU.mult, [B_bank[bk], B_const], [B_cbTm[p]])

                        return [lambda: xs_half(0), lambda: xs_half(1), bcb]

                    def ssd_M(c, hook=None):
                        p = c % 2
                        csl = slice(c * 128, (c + 1) * 128)
                        st = {}

                        def m0(hq):
                            rs = rseg[:, hq % 2, :].rearrange("p (h l) -> p h l", h=8)
                            tt_("pool", rs, U_f[:].unsqueeze(1).to_broadcast([128, 8, 128]),
                                hsl(c, 2, hq * 8, 8).unsqueeze(2).to_broadcast([128, 8, 128]), ALU.mult,
                                [B_const, B_dts], [B_rseg[hq % 2]])

                        def m1(hq):
                            mo = (hq % 2) * 8
                            for q in range(2):
                                bk = nbank((0, 1, 2, 3))
                                mm(banks[bk][:, :], Ls_f[:], rseg[:, hq % 2, q * 512:(q + 1) * 512], True, True,
                                   [B_rseg[hq % 2], B_const], bk)
                                act(MT[:, mo + q * 4:mo + q * 4 + 4, :].rearrange("p h l -> p (h l)"),
                                    banks[bk][:, :], AF.Exp, [B_bank[bk]], [B_MT[hq % 2]])

                        def m2(hq):
                            mo = (hq % 2) * 8
                            g = hq // 2
                            tt_("dve", MT[:, mo:mo + 8, :], MT[:, mo:mo + 8, :],
                                cbTm[:, p, g * 128:(g + 1) * 128].unsqueeze(1).to_broadcast([128, 8, 128]), ALU.mult,
                                [B_MT[hq % 2], B_cbTm[p]], [B_MT[hq % 2]])

                        def m3(hq):
                            mo = (hq % 2) * 8
                            g = hq // 2
                            ybk = nbank((0, 1, 2, 3))
                            for i in range(8):
                                h = hq * 8 + i
                                mm(banks[ybk][:, i * 64:(i + 1) * 64], MT[:, mo + i, :], xdt[:, p, h * 64:(h + 1) * 64], True, True,
                                   [B_MT[hq % 2], B_xdt[p]], ybk)
                            obk = nbank((0, 1, 2, 3))
                            mm(banks[obk][:, :], xbcT[:, 18 + g, csl], STb[:, hq * 512:(hq + 1) * 512], True, True,
                               [B_xbc[18 + g], B_STb[hq]], obk)
                            yt, Byt = ytmp[:, hq % 2, :], B_ytmp[hq % 2]
                            tt_("dve", yt.rearrange("p (h d) -> p h d", d=64),
                                banks[obk][:, :].rearrange("p (h d) -> p h d", d=64),
                                hsl(c, 5, hq * 8, 8).unsqueeze(2).to_broadcast([128, 8, 64]), ALU.mult,
                                [B_bank[obk], B_dts], [Byt])
                            tt_("dve", ytok[:, p, hq * 512:(hq + 1) * 512], banks[ybk][:, :], yt, ALU.add,
                                [B_bank[ybk], Byt], [B_ytok[p * 4 + hq]])

                        pipeline(list(range(4)), [m0, m1, m2, m3], hook=hook)

                    def ssd_S(c):
                        p = c % 2
                        for hq in range(4):
                            g = hq // 2
                            bk = nbank((0, 1, 2, 3))
                            mm(banks[bk][:, :], Btok[:, p, g * 128:(g + 1) * 128], xdtd[:, p, hq * 512:(hq + 1) * 512], True, True,
                               [B_Btok[p], B_xdtd[p]], bk)
                            stv = ST[:, hq * 512:(hq + 1) * 512]
                            tt_("pool", stv.rearrange("p (h d) -> p h d", d=64), stv.rearrange("p (h d) -> p h d", d=64),
                                hsl(c, 6, hq * 8, 8).unsqueeze(2).to_broadcast([128, 8, 64]), ALU.mult, [B_ST[hq], B_dts], [B_ST[hq]])
                            tt_("dve", stv, stv, banks[bk][:, :], ALU.add, [B_ST[hq], B_bank[bk]], [B_ST[hq]])
                            act(STb[:, hq * 512:(hq + 1) * 512], stv, AF.Copy, [B_ST[hq]], [B_STb[hq]])

                    def ssd_T_thunks(c):
                        p = c % 2
                        csl = slice(c * 128, (c + 1) * 128)
                        th = []
                        for hh in range(2):
                            stt = {}

                            def tr(hh=hh, stt=stt):
                                bk = nbank((6, 7))
                                stt["bk"] = bk
                                pbf = banks[bk][:, :].bitcast(BF16)
                                for i in range(8):
                                    cb = hh * 8 + i
                                    transpose(pbf[:, i * 128:(i + 1) * 128], ytok[:, p, cb * 128:(cb + 1) * 128], [B_ytok[p * 4 + cb // 4]], bk)
                            th.append(tr)
                            for i2 in range(4):
                                def grp(hh=hh, i2=i2, stt=stt):
                                    bk = stt["bk"]
                                    pbf = banks[bk][:, :].bitcast(BF16)
                                    pair = [(hh * 8 + i2 * 2 + d_, i2 * 2 + d_) for d_ in range(2)]
                                    for cb, i in pair:
                                        stt_("dve", tA[:, cb % 2, :], xbcT[:, cb, csl], dskip[:, cb:cb + 1], pbf[:, i * 128:(i + 1) * 128],
                                             ALU.mult, ALU.add, [B_xbc[cb], B_par, B_bank[bk]], [B_tA[cb % 2]])
                                    for cb, i in pair:
                                        stt_("dve", zs[:, cb, csl], tA[:, cb % 2, :], ssmnw[:, cb:cb + 1], zs[:, cb, csl], ALU.mult, ALU.mult,
                                             [B_tA[cb % 2], B_par, B_zs[cb]], [B_zs[cb]])
                                    for cb, i in pair:
                                        act(sq_[:, cb % 2, :], zs[:, cb, csl], AF.Square, [B_zs[cb], B_par], [B_sq[cb % 2]],
                                            scale=inv_nw[:, cb:cb + 1])
                                    for cb, i in pair:
                                        mm(banks[ssbk][:, csl], ones_b[:], sq_[:, cb % 2, :], cb == 0, cb == 15, [B_sq[cb % 2], B_const], ssbk)
                                th.append(grp)
                        return th

                    def kv_thunks(blk):
                        def kpart():
                            def ev(m, bk, blk=blk):
                                cb = blk * 2 + m
                                cp_("dve", Kc[:, cb, tt * TT:(tt + 1) * TT], banks[bk][:, :], [B_bank[bk]], [B_Kc[tt]])
                            proj_fm(w_in[l][:, C_K + blk * 256:C_K + (blk + 1) * 256], 8, 256, rhs_h, [B_hT], ev)

                        def vpart():
                            wv, wb = load_w(w_in[l][:, C_V + blk * 256:C_V + (blk + 1) * 256], 8, 256)
                            for jp in range(2):
                                bk = nbank((0, 1, 2, 3))
                                for jj in range(2):
                                    j = jp * 2 + jj
                                    for kc in range(8):
                                        mm(banks[bk][:, jj * 256:(jj + 1) * 256], hT[:, kc, j * 128:(j + 1) * 128], wv[:, kc, :],
                                           kc == 0, kc == 7, [wb, B_hT], bk)
                                cp_("dve", Vc[:, tt * 4 + jp * 2:tt * 4 + jp * 2 + 2, blk * 256:(blk + 1) * 256],
                                    banks[bk][:, :].rearrange("p (j c) -> p j c", j=2), [B_bank[bk]], [B_Vc[tt]])
                        return [kpart, vpart]

                    for th_ in ssd_F_thunks(0):
                        th_()
                    for c in range(4):
                        fill = []
                        if c >= 1:
                            fill += ssd_T_thunks(c - 1)
                        if c + 1 < 4:
                            fill += ssd_F_thunks(c + 1)
                        fill += kv_thunks(c)
                        nst = 7
                        per = (len(fill) + nst - 1) // nst

                        def hook(step, fill=fill, per=per):
                            for _ in range(per):
                                if fill:
                                    fill.pop(0)()
                        ssd_M(c, hook)
                        ssd_S(c)
                        while fill:
                            fill.pop(0)()
                    for th_ in ssd_T_thunks(3):
                        th_()
                    act(rstdb[:], banks[ssbk][:, :], AF.Ln, [B_bank[ssbk]], [B_rstdb], scale=1.0 / SSMW, bias=EPS)
                    act(rstdb[:], rstdb[:], AF.Exp, [B_rstdb], [B_rstdb], scale=-0.5)
                    tap("ygn", zs[:].rearrange("p k t -> p (k t)"), B_zs[15], [128, 16 * TT])
                    tap("rstdb", rstdb[:], B_rstdb, [128, TT])
                    fz = all_tickets(ar["bufs"])
                    for b_ in B_mg:
                        b_.r = list(fz) + list(b_.r)
                    for mb in range(8):
                        gbuf, Bg = gate_block(mb, 0)

                        def ev(m, bk, mb=mb, gbuf=gbuf, Bg=Bg):
                            tt_("dve", t1[mb % 2][:], banks[bk][:, :], rstdb[:], ALU.mult, [B_bank[bk], B_rstdb], [B_t1[mb % 2]])
                            tt_("dve", mg[:, mb, :], t1[mb % 2][:], gbuf[:], ALU.mult, [B_t1[mb % 2], Bg], [B_mg[mb]])
                        proj_fm(w_ps[l][:, mb * 128:(mb + 1) * 128], 16, 128, lambda kc: zs[:, kc, :], B_zs, ev)
                    tap("mg0", mg[:].rearrange("p k t -> p (k t)"), B_mg[7], [128, 8 * TT])

                    ar_reset()
                    ub, B_ub = ar_alloc("ub", [128, 8, TT + 16], F32, 8)
                    pa, B_pa = ar_alloc("pa", [128, 2, TT + 16], F32, 2)
                    pb_, B_pb = ar_alloc("pb", [128, 2, TT + 16], F32, 2)
                    mixed, B_mixed = ar_alloc("mixed", [128, 8, TT], BF16, 8)
                    sgp, B_sgp = ar_alloc("sgp", [128, 8, TT], BF16, 8)
                    pooled, B_pooled = ar_alloc("pooled", [128, 8, TT], BF16, 8)
                    for blk in range(4):
                        def ev(m, bk, blk=blk):
                            cb = blk * 2 + m
                            act(ub[:, cb, 16:TT + 16], banks[bk][:, :], AF.Copy, [B_bank[bk]], [B_ub[cb]])
                        proj_fm(w_in[l][:, C_PU + blk * 256:C_PU + (blk + 1) * 256], 8, 256, rhs_h, [B_hT], ev)
                    for blk in range(4):
                        def ev(m, bk, blk=blk):
                            cb = blk * 2 + m
                            act(sgp[:, cb, :], banks[bk][:, :], AF.Silu, [B_bank[bk]], [B_sgp[cb]])
                        proj_fm(w_in[l][:, C_PG + blk * 256:C_PG + (blk + 1) * 256], 8, 256, rhs_h, [B_hT], ev)
                    for cb in range(8):
                        g = cb // 2
                        if first_tile:
                            mset("dve", ub[:, cb, 0:16], 0.0, [B_ub[cb]])
                        else:
                            cp_("dve", ub[:, cb, 0:16], phalo[:, cb, :], [B_phalo[cb]], [B_ub[cb]])
                        W_ = TT + 16
                        cur, Bcur = ub[:, cb, :], B_ub[cb]
                        tmp = [(pa[:, cb % 2, :], B_pa[cb % 2]), (pb_[:, cb % 2, :], B_pb[cb % 2])]
                        for lev in range(g + 1):
                            sh = 1 << lev
                            dst, Bdst = tmp[lev % 2]
                            tt_("dve", dst[:, sh:W_], cur[:, sh:W_], cur[:, 0:W_ - sh], ALU.add, [Bcur], [Bdst])
                            if lev > 0:
                                pass
                            cur, Bcur = dst, Bdst
                        w = 2 << g
                        stt_("dve", mixed[:, cb, :], cur[:, 16:W_], 1.0 / w, ub[:, cb, 16:W_], ALU.mult, ALU.subtract,
                             [Bcur, B_ub[cb]], [B_mixed[cb]])
                        if first_tile:
                            tt_("dve", cur[:, 16:32], cur[:, 16:32], invcnt[:, g, :], ALU.mult, [Bcur, B_const], [Bcur])
                            tt_("dve", mixed[:, cb, 0:16], cur[:, 16:32], ub[:, cb, 16:32], ALU.subtract,
                                [Bcur, B_ub[cb]], [B_mixed[cb]])
                        cp_("dve", phalo[:, cb, :], ub[:, cb, TT:TT + 16], [B_ub[cb]], [B_phalo[cb]])
                    for g in range(4):
                        wv, wb = load_w(pool_w[l][g], 2, 256)
                        for m in range(2):
                            bk = nbank((0, 1, 2, 3))
                            for kc in range(2):
                                mm(banks[bk][:, :], wv[:, kc, m * 128:(m + 1) * 128], mixed[:, 2 * g + kc, :], kc == 0, kc == 1,
                                   [wb, B_mixed[2 * g + kc]], bk)
                            cb = 2 * g + m
                            stt_("dve", pooled[:, cb, :], banks[bk][:, :], pscale[:, cb:cb + 1], sgp[:, cb, :], ALU.mult, ALU.mult,
                                 [B_bank[bk], B_par, B_sgp[cb]], [B_pooled[cb]])
                    tap("pooled", pooled[:].rearrange("p k t -> p (k t)"), B_pooled[7], [128, 8 * TT])
                    for mb in range(8):
                        gbuf, Bg = gate_block(mb, 1)

                        def ev(m, bk, mb=mb, gbuf=gbuf, Bg=Bg):
                            tt_("dve", t1[mb % 2][:], banks[bk][:, :], gbuf[:], ALU.mult, [B_bank[bk], Bg], [B_t1[mb % 2]])
                            tt_("dve", mg[:, mb, :], mg[:, mb, :], t1[mb % 2][:], ALU.add, [B_t1[mb % 2], B_mg[mb]], [B_mg[mb]])
                        proj_fm(w_pp[l][:, mb * 128:(mb + 1) * 128], 8, 128, lambda kc: pooled[:, kc, :], B_pooled, ev)

                    ar_reset()
                    qT, B_qT = ar_alloc("qT", [128, 8, TT], BF16, 8)
                    sgb, B_sgb = ar_alloc("sgb", [128, 8, TT], BF16, 8)
                    sbo, B_sbo = ar_alloc("sbo", [128, 8, TT], BF16, 8)
                    NE = 4
                    e_sb, B_e = ar_alloc("e_sb", [128, NE, TT], F32, NE)
                    sp_sb, B_sp = ar_alloc("sp_sb", [128, 4, TT], BF16, 4)
                    tmp_sb, B_tmp = ar_alloc("tmp_sb", [128, NE, TT], F32, NE)
                    att_sb, B_att = ar_alloc("att_sb", [128, NE, TT], BF16, NE)
                    carry, B_carry = ar_alloc("carry", [128, 4, TT], F32, 4)
                    g2all, B_g2 = ar_alloc("g2all", [128, 8, TT], BF16, 8)
                    has_next = not (sq == NSEQ - 1 and tt == NTT - 1)
                    if has_next:
                        nxblk, B_nxblk = ar_alloc("nxblk", [128, 2, D], F32, 2)
                        nhb, B_nhb = ar_alloc("nhb", [128, 2, D], BF16, 2)
                        ntok0 = tok0 + TT
                    for blk in range(4):
                        def ev(m, bk, blk=blk):
                            cb = blk * 2 + m
                            pl.op("act", lambda e, o=qT[:, cb, :], i_=banks[bk][:, :]: e.mul(o, i_, 0.125), [B_bank[bk]], [B_qT[cb]])
                        proj_fm(w_in[l][:, C_Q + blk * 256:C_Q + (blk + 1) * 256], 8, 256, rhs_h, [B_hT], ev)
                    for blk in range(4):
                        def ev(m, bk, blk=blk):
                            cb = blk * 2 + m
                            act(sgb[:, cb, :], banks[bk][:, :], AF.Silu, [B_bank[bk]], [B_sgb[cb]])
                        proj_fm(w_in[l][:, C_SG + blk * 256:C_SG + (blk + 1) * 256], 8, 256, rhs_h, [B_hT], ev)
                    nkt = (tt + 1) * 4
                    items = []
                    for hp in range(8):
                        for a in reversed(range(nkt)):
                            items.append({"hp": hp, "a": a, "i": len(items)})
                    for it in items:
                        hp, a, i = it["hp"], it["a"], it["i"]
                        r = a - tt * 4
                        it["c0"] = max(r, 0) * 128
                        it["diag"] = r >= 0
                        it["first"] = (a == nkt - 1)
                        it["last"] = (a == 0)
                        it["zb"] = [(i % 2) * 2, (i % 2) * 2 + 1]
                        it["tb"] = [4, 5]
                        it["ob"] = 6 + (hp % 2)
                        it["e"] = [(i % 2) * 2, (i % 2) * 2 + 1]
                        it["sp"] = [(i % 2) * 2, (i % 2) * 2 + 1]
                        it["cy"] = [(hp % 2) * 2, (hp % 2) * 2 + 1]
                        it["kreads"] = [B_Kc[a // 4]]
                        it["vreads"] = [B_Vc[a // 4]]

                    def s0(it):
                        c0, hp, a = it["c0"], it["hp"], it["a"]
                        for hh in range(2):
                            zb, po = it["zb"][hh], 64 * hh
                            mm(banks[zb][:, c0:TT], Kc[po:po + 64, hp, a * 128:(a + 1) * 128], qT[po:po + 64, hp, c0:TT],
                               True, not it["diag"], it["kreads"] + [B_qT[hp]], zb)
                        if it["diag"]:
                            for hh in range(2):
                                zb = it["zb"][hh]
                                mm(banks[zb][:, c0:c0 + 128], ident_b[:], NEG_b[:], False, True, [B_const], zb, skip_group_check=True)

                    def s1(it):
                        c0 = it["c0"]
                        z0, e0, p0 = it["zb"][0], it["e"][0], it["sp"][0]
                        act(e_sb[:, e0:e0 + 2, c0:TT], ps_all[:, z0:z0 + 2, c0:TT], AF.Exp, [B_bank[z0], B_bank[z0 + 1]],
                            [B_e[e0], B_e[e0 + 1]])
                        act(sp_sb[:, p0:p0 + 2, c0:TT], e_sb[:, e0:e0 + 2, c0:TT], AF.Ln, [B_e[e0], B_e[e0 + 1]],
                            [B_sp[p0], B_sp[p0 + 1]], bias=1.0)

                    def s2(it):
                        c0 = it["c0"]
                        for hh in range(2):
                            zb, si = it["zb"][hh], it["sp"][hh]
                            mm(banks[zb][:, c0:TT], nLi_b[:], sp_sb[:, si, c0:TT], False, True, [B_sp[si], B_const], zb,
                               skip_group_check=True)
                        if not it["last"]:
                            for hh in range(2):
                                tb, si = it["tb"][hh], it["sp"][hh]
                                mm(banks[tb][:, c0:TT], nOnes_b[:], sp_sb[:, si, c0:TT], True, True, [B_sp[si], B_const], tb)

                    def s3(it):
                        c0 = it["c0"]
                        z0, e0, cy0 = it["zb"][0], it["e"][0], it["cy"][0]
                        Bcy = [B_carry[cy0], B_carry[cy0 + 1]]
                        if it["first"]:
                            mset("dve", carry[:, cy0:cy0 + 2, :], 0.0, Bcy)
                        tt_("dve", tmp_sb[:, e0:e0 + 2, c0:TT], ps_all[:, z0:z0 + 2, c0:TT], carry[:, cy0:cy0 + 2, c0:TT], ALU.add,
                            [B_bank[z0], B_bank[z0 + 1]] + Bcy, [B_tmp[e0], B_tmp[e0 + 1]])
                        if not it["last"]:
                            tt_("dve", carry[:, cy0:cy0 + 2, c0:TT], ps_all[:, 4:6, c0:TT], carry[:, cy0:cy0 + 2, c0:TT], ALU.add,
                                [B_bank[4], B_bank[5]] + Bcy, Bcy)

                    def s4(it):
                        c0 = it["c0"]
                        e0 = it["e"][0]
                        act(att_sb[:, e0:e0 + 2, c0:TT], tmp_sb[:, e0:e0 + 2, c0:TT], AF.Exp, [B_tmp[e0], B_tmp[e0 + 1]],
                            [B_att[e0], B_att[e0 + 1]])

                    def s5(it):
                        c0, ob, a, hp = it["c0"], it["ob"], it["a"], it["hp"]
                        for hh in range(2):
                            ei, po, h = it["e"][hh], 64 * hh, 2 * hp + hh
                            mm(banks[ob][po:po + 64, c0:TT], Vc[:, a, h * 64:(h + 1) * 64], att_sb[:, ei, c0:TT],
                               it["first"], it["last"], it["vreads"] + [B_att[ei]], ob, skip_group_check=True)
                        if it["last"]:
                            tt_("dve", sbo[:, hp, :], banks[ob][:, :], sgb[:, hp, :], ALU.mult, [B_bank[ob], B_sgb[hp]], [B_sbo[hp]])

                    for mb in range(0, 8, 2):
                        c0g = C_MG + 2 * D + mb * 128

                        def evg(m, bk, mb=mb):
                            act(g2all[:, mb + m, :], banks[bk][:, :], AF.Sigmoid, [B_bank[bk]], [B_g2[mb + m]])
                        proj_fm(w_in[l][:, c0g:c0g + 256], 8, 256, rhs_h, [B_hT], evg)
                    hook = None
                    if has_next:
                        nsteps = len(items) + 4
                        at = {max(2, (nsteps * (jn + 1)) // 6): jn for jn in range(4)}

                        def hook(step):
                            if step in at:
                                norm_block(ntok0, at[step], nxblk, B_nxblk, nhb, B_nhb, tpool=(4, 5))
                    pipeline(items, [s0, s1, lambda it: (s2(it), s3(it)), s4, s5], order=[2, 4, 3, 1, 0], hook=hook)
                    if has_next:
                        prefetched["v"] = True
                    tap("sbo", sbo[:].rearrange("p k t -> p (k t)"), B_sbo[7], [128, 8 * TT])
                    for mb in range(8):
                        gbuf, Bg = g2all[:, mb, :], B_g2[mb]

                        def ev(m, bk, mb=mb, gbuf=gbuf, Bg=Bg):
                            tt_("dve", t1[mb % 2][:], banks[bk][:, :], gbuf, ALU.mult, [B_bank[bk], Bg], [B_t1[mb % 2]])
                            tt_("dve", mg[:, mb, :], mg[:, mb, :], t1[mb % 2][:], ALU.add, [B_t1[mb % 2], B_mg[mb]], [B_mg[mb]])
                        proj_fm(w_pb[l][:, mb * 128:(mb + 1) * 128], 8, 128, lambda kc: sbo[:, kc, :], B_sbo, ev)

                    ar_reset()
                    mgb, B_mgb = ar_alloc("mgb", [128, 8, TT], BF16, 8)
                    xo, B_xo = ar_alloc("xo", [128, 4, D], F32, 4)
                    if last and final:
                        junk, B_junk = ar_alloc("junk", [128, D], BF16, 1)
                        fnw, B_fnw = ar_alloc("fnw", [128, D], F32, 1)
                        pl.dma("sp", fnw, p_fnw, [], [B_fnw[0]], B_fnw[0])
                    for mb in range(8):
                        cp_("dve", mgb[:, mb, :], mg[:, mb, :], [B_mg[mb]], [B_mgb[mb]])
                    for j in range(4):
                        r0 = tok0 + j * 128
                        pl.dma("sp", xo[:, j, :], src_x[r0:r0 + 128, :], [xbuf(src_id, r0)], [B_xo[j]], B_xo[j])
                    for half in range(2):
                        bks = [nbank((0, 1, 2, 3)) for j in range(4)]
                        for kh in range(2):
                            wv, wb = load_w(w_out[l][kh * 512:(kh + 1) * 512, half * 512:(half + 1) * 512], 4, 512)
                            for j in range(4):
                                for k4 in range(4):
                                    kc = kh * 4 + k4
                                    mm(banks[bks[j]][:, :], mgb[:, kc, j * 128:(j + 1) * 128], wv[:, k4, :], kc == 0, kc == 7,
                                       [wb, B_mgb[kc]], bks[j])
                        for j in range(4):
                            xv = xo[:, j, half * 512:(half + 1) * 512]
                            tt_("dve", xv, banks[bks[j]][:, :], xv, ALU.add, [B_bank[bks[j]], B_xo[j]], [B_xo[j]])
                    for j in range(4):
                        r0 = tok0 + j * 128
                        xov, Bxo = xo[:, j, :], B_xo[j]
                        if last and final:
                            ssj, Bss = sst[:, (j % 2) * 4:(j % 2) * 4 + 1], B_sst[j % 2]
                            rsj = sst[:, (j % 2) * 4 + 1:(j % 2) * 4 + 2]
                            mset("dve", ssj, 0.0, [Bss])
                            act(junk, xov, AF.Square, [Bxo, Bss], [B_junk[0], Bss], accum_out=ssj)
                            act(rsj, ssj, AF.Ln, [Bss], [Bss], scale=1.0 / D, bias=EPS)
                            act(rsj, rsj, AF.Exp, [Bss], [Bss], scale=-0.5)
                            stt_("dve", xov, xov, rsj, fnw, ALU.mult, ALU.mult, [Bxo, Bss, B_fnw[0]], [Bxo])
                        pl.dma("sp", dst_x[r0:r0 + 128, :], xov, [Bxo], [xbuf(dst_id, r0)], Bxo)
                        if last:
                            outs_final.append(xbuf(dst_id, r0))
        pl.final_wait("sp", outs_final + [v[1] for v in tapd.values()])
        pl.emit()
    return nc, pl


def _prep_inputs(inp, NL, l0=0):
    f = lambda a: np.ascontiguousarray(np.asarray(a, dtype=np.float32))
    sl = slice(l0, l0 + NL)
    m = {}
    m["w_in"] = f(inp["w_in"][sl])
    m["w_ps"] = f(inp["w_proj_ssm"][sl])
    m["w_pp"] = f(inp["w_proj_pool"][sl])
    m["w_pb"] = f(inp["w_proj_sb"][sl])
    m["w_out"] = f(inp["w_out"][sl])
    m["pool_w"] = f(inp["pool_w"][sl])
    rep = lambda a: f(np.broadcast_to(np.asarray(a)[:, None, :], (NL, 128, np.asarray(a).shape[-1])))
    col = lambda a, n: f(np.asarray(a).reshape(NL, n, 128).transpose(0, 2, 1))
    m["p_normw"] = rep(inp["norm_w"][sl])
    cw = np.asarray(inp["conv_w"][sl])
    m["p_convw"] = f(cw.transpose(0, 2, 1).reshape(NL, 20, 128, 4).transpose(0, 2, 1, 3).reshape(NL, 128, 80))
    m["p_convb"] = col(inp["conv_b"][sl], 20)
    m["p_dtb"] = rep(inp["dt_bias"][sl])
    m["p_alog"] = rep(inp["a_log"][sl])
    m["p_dskip"] = col(np.repeat(np.asarray(inp["d_skip"][sl]), 64, axis=1), 16)
    m["p_ssmnw"] = col(inp["ssm_norm_w"][sl], 16)
    m["p_pscale"] = col(inp["pool_scale"][sl], 8)
    m["p_fnw"] = f(np.broadcast_to(np.asarray(inp["final_norm_w"])[None, :], (128, D)))
    m.update(host_consts())
    return m


_CACHE = {}


def kernel(**inputs):
    x = np.asarray(inputs["x"], dtype=np.float32)
    Bsz, S, _ = x.shape
    NL = inputs["w_in"].shape[0]
    NCORE = 8
    NSEQ = Bsz // NCORE
    key = (S, NSEQ, NL)
    if key not in _CACHE:
        _CACHE[key] = build(S, NSEQ, NL, True)[0]
    nc = _CACHE[key]
    shared = _prep_inputs(inputs, NL)
    in_maps = []
    for c in range(NCORE):
        mp = dict(shared)
        mp["x"] = np.ascontiguousarray(x[c * NSEQ:(c + 1) * NSEQ].reshape(NSEQ * S, D))
        in_maps.append(mp)
    res = run_bass_kernel_spmd(nc, in_maps, core_ids=list(range(NCORE)))
    out = np.concatenate([r["out"].reshape(NSEQ, S, D) for r in res.results], axis=0)
    return out.astype(np.float32)
```

```python
import contextlib
import numpy as np
import concourse.bass as bass
import concourse.mybir as mybir
from concourse.bass_utils import run_bass_kernel_spmd

F32 = mybir.dt.float32
BF16 = mybir.dt.bfloat16
AF = mybir.ActivationFunctionType
ALU = mybir.AluOpType

D = 1024
SSMW = 2048
NH = 32
CONVC = 2560
INC = 13856
C_Z, C_XBC, C_DT, C_PU, C_PG, C_Q, C_K, C_V, C_SG, C_MG = 0, 2048, 4608, 4640, 5664, 6688, 7712, 8736, 9760, 10784
EPS = 1e-6
TT = 512
SAME_ENGINE_SYNC = True
NO_SELF_SYNC = ("pe",)


class Buf:
    __slots__ = ("w", "r", "sem", "name")

    def __init__(self, name=""):
        self.w = None
        self.r = []
        self.sem = None
        self.name = name


class SemH:
    def __init__(self, h, name):
        self.h = h
        self.name = name
        self.total = 0


class Plan:
    ENGS = ("pe", "dve", "act", "pool", "sp")

    def __init__(self, nc, stack):
        self.nc = nc
        self.stack = stack
        self.streams = {e: [] for e in self.ENGS}
        self.cnt = {e: 0 for e in self.ENGS}
        self.waited = {e: {} for e in self.ENGS}
        self.nsem = 0
        self.esem = {e: self.new_sem("e_" + e) for e in self.ENGS}
        self.ninst = 0
        self.semcache = {}

    def new_sem(self, name):
        self.nsem += 1
        return SemH(self.stack.enter_context(self.nc.semaphore(name)), name)

    def _wait(self, eng, tickets):
        need = {}
        for (sem, val, isdma) in tickets:
            if isdma:
                val = sem.total
            elif sem is self.esem[eng] and (eng in NO_SELF_SYNC or not SAME_ENGINE_SYNC):
                continue
            if need.get(sem, 0) < val:
                need[sem] = val
        for sem, val in need.items():
            if self.waited[eng].get(sem, 0) < val:
                self.streams[eng].append(("w", sem, val))
                self.waited[eng][sem] = val

    def _deps(self, reads, writes):
        t = []
        for b in reads:
            if b.w is not None:
                t.append(b.w)
        for b in writes:
            if b.w is not None:
                t.append(b.w)
            t.extend(b.r)
        return t

    def op(self, eng, fn, reads=(), writes=()):
        ex = [b for b in reads if b.name.startswith("bank")]
        if ex:
            reads = [b for b in reads if not b.name.startswith("bank")]
            writes = list(writes) + ex
        self._wait(eng, self._deps(reads, writes))
        sem = self.esem[eng]
        self.cnt[eng] += 1
        sem.total = self.cnt[eng]
        tk = (sem, self.cnt[eng], False)
        self.streams[eng].append(("i", fn, sem, 1))
        for b in reads:
            b.r.append(tk)
        for b in writes:
            b.w = tk
            b.r = []
        self.ninst += 1

    def dma(self, eng, out_ap, in_ap, reads, writes, owner):
        self._wait(eng, self._deps(reads, writes))
        if owner.sem is None:
            if owner.name not in self.semcache:
                self.semcache[owner.name] = self.new_sem("d_" + owner.name)
            owner.sem = self.semcache[owner.name]
        sem = owner.sem
        sem.total += 16
        tk = (sem, sem.total, True)
        self.streams[eng].append(("i", lambda e: e.dma_start(out=out_ap, in_=in_ap), sem, 16))
        for b in reads:
            b.r.append(tk)
        for b in writes:
            b.w = tk
            b.r = []
        self.ninst += 1

    def final_wait(self, eng, bufs):
        t = []
        for b in bufs:
            if b.w is not None:
                t.append(b.w)
            t.extend(b.r)
        self._wait(eng, t)

    def emit(self):
        nc = self.nc
        streams = self.streams

        def replay(name, e):
            for rec in streams[name]:
                if rec[0] == "w":
                    e.wait_ge(rec[1].h, rec[2])
                else:
                    rec[1](e).then_inc(rec[2].h, rec[3])

        with nc.Block() as block:
            @block.tensor
            def _(e):
                replay("pe", e)

            @block.vector
            def _(e):
                replay("dve", e)

            @block.scalar
            def _(e):
                replay("act", e)

            @block.gpsimd
            def _(e):
                replay("pool", e)

            @block.sync
            def _(e):
                replay("sp", e)


def pipeline(items, stages, order=None, hook=None):
    n, d = len(items), len(stages)
    order = list(reversed(range(d))) if order is None else order
    for step in range(n + d - 1):
        if hook is not None:
            hook(step)
        for si in order:
            st = stages[si]
            k = step - si
            if 0 <= k < n:
                st(items[k])


def host_consts():
    i = np.arange(128)
    c = {}
    c["c_ident"] = np.eye(128, dtype=np.float32)
    c["c_U"] = (i[:, None] <= i[None, :]).astype(np.float32)
    c["c_Ls"] = (i[:, None] > i[None, :]).astype(np.float32)
    c["c_ones"] = np.ones((128, 128), np.float32)
    c["c_nLi"] = -(i[:, None] >= i[None, :]).astype(np.float32)
    c["c_nOnes"] = -np.ones((128, 128), np.float32)
    c["c_NEG"] = -30000.0 * (i[:, None] >= i[None, :]).astype(np.float32)
    inv = np.zeros((128, 4, 16), np.float32)
    for g, w in enumerate((2, 4, 8, 16)):
        inv[:, g, :] = 1.0 / np.minimum(np.arange(16) + 1, w)
    c["c_invcnt"] = inv.reshape(128, 64)
    return c


def build(S=2048, NSEQ=2, NL=4, final=True, taps=()):
    nc = bass.Bass("TRN2", target_bir_lowering=False)
    NT = NSEQ * S
    NTT = S // TT
    stack = contextlib.ExitStack()
    with stack:
        pl = Plan(nc, stack)

        def dram(name, shape, dt=F32, kind="ExternalInput"):
            return nc.dram_tensor(name, list(shape), dt, kind=kind).ap()

        x_in = dram("x", [NT, D])
        y_out = dram("out", [NT, D], kind="ExternalOutput")
        w_in = dram("w_in", [NL, D, INC])
        w_ps = dram("w_ps", [NL, SSMW, D])
        w_pp = dram("w_pp", [NL, D, D])
        w_pb = dram("w_pb", [NL, D, D])
        w_out = dram("w_out", [NL, D, D])
        pool_w = dram("pool_w", [NL, 4, 256, 256])
        p_normw = dram("p_normw", [NL, 128, D])
        p_convw = dram("p_convw", [NL, 128, 80])
        p_convb = dram("p_convb", [NL, 128, 20])
        p_dtb = dram("p_dtb", [NL, 128, 32])
        p_alog = dram("p_alog", [NL, 128, 32])
        p_dskip = dram("p_dskip", [NL, 128, 16])
        p_ssmnw = dram("p_ssmnw", [NL, 128, 16])
        p_pscale = dram("p_pscale", [NL, 128, 8])
        p_fnw = dram("p_fnw", [128, D])
        cdr = {k: dram(k, v.shape) for k, v in host_consts().items()}
        xscr = [dram("xscr%d" % i, [NT, D], kind="Internal") for i in range(2)] if NL > 1 else []
        tapd = {}

        def sb(name, shape, dt):
            return stack.enter_context(nc.sbuf_tensor(name, list(shape), dt))

        cst_f = {k: sb("f" + k, [128, 128], F32) for k in ("c_U", "c_Ls", "c_ones")}
        cst_b = {k: sb("b" + k, [128, 128], BF16) for k in ("c_ident", "c_ones", "c_nLi", "c_nOnes", "c_NEG")}
        invcnt = sb("invcnt", [128, 4, 16], F32)
        B_const = Buf("const")
        for k, t in cst_f.items():
            pl.dma("sp", t[:], cdr[k], [], [B_const], B_const)
        for k, t in cst_b.items():
            pl.dma("pool", t[:], cdr[k], [], [B_const], B_const)
        pl.dma("sp", invcnt[:].rearrange("p a b -> p (a b)"), cdr["c_invcnt"], [], [B_const], B_const)
        U_f, Ls_f, ones_f = cst_f["c_U"], cst_f["c_Ls"], cst_f["c_ones"]
        ident_b, ones_b, nLi_b, nOnes_b, NEG_b = (cst_b[k] for k in ("c_ident", "c_ones", "c_nLi", "c_nOnes", "c_NEG"))

        normw = sb("normw", [128, D], F32)
        convw = sb("convw", [128, 20, 4], F32)
        convb = sb("convb", [128, 20], F32)
        dtb = sb("dtb", [128, 32], F32)
        a_neg = sb("a_neg", [128, 32], F32)
        dskip = sb("dskip", [128, 16], F32)
        ssmnw = sb("ssmnw", [128, 16], F32)
        inv_nw = sb("inv_nw", [128, 16], F32)
        pscale = sb("pscale", [128, 8], F32)
        B_par = Buf("par")

        Kc = sb("Kc", [128, 8, S], BF16)
        Vc = sb("Vc", [128, S // 128, D], BF16)
        ST = sb("ST", [128, 2048], F32)
        STb = sb("STb", [128, 2048], BF16)
        chalo = sb("chalo", [128, 20, 3], F32)
        B_Kc = [Buf("Kc%d" % i) for i in range(NTT)]
        B_Vc = [Buf("Vc%d" % i) for i in range(NTT)]
        phalo = sb("phalo", [128, 8, 16], F32)
        B_ST = [Buf("ST%d" % i) for i in range(4)]
        B_STb = [Buf("STb%d" % i) for i in range(4)]
        B_chalo = [Buf("chalo%d" % i) for i in range(20)]
        B_phalo = [Buf("phalo%d" % i) for i in range(8)]

        sst = sb("sst", [128, 8], F32)
        B_sst = [Buf("sst%d" % i) for i in range(2)]
        hT = sb("hT", [128, 8, TT], BF16)
        B_hT = Buf("hT")
        NSLOT = 3
        wsl = [sb("wsl%d" % i, [128, 2048], BF16) for i in range(NSLOT)]
        B_wsl = [Buf("wsl%d" % i) for i in range(NSLOT)]
        B_mg = [Buf("mg%d" % i) for i in range(8)]
        gate = [sb("gate%d" % i, [128, TT], BF16) for i in range(2)]
        B_gate = [Buf("gate%d" % i) for i in range(2)]
        t1 = [sb("t1_%d" % i, [128, TT], F32) for i in range(2)]
        B_t1 = [Buf("t1_%d" % i) for i in range(2)]
        ARENA_F = 24064
        MG_OFF = ARENA_F - 8 * TT
        arena = sb("arena", [128, ARENA_F], F32)
        mg = arena[:, MG_OFF:ARENA_F].rearrange("p (a b) -> p a b", a=8)
        ar = {"off": 0, "bufs": [], "fence": [], "limit": MG_OFF}

        def all_tickets(bufs):
            f = []
            for b in bufs:
                if b.w is not None:
                    f.append(b.w)
                f.extend(b.r)
            best = {}
            for (s_, v, d_) in f:
                if s_ not in best or best[s_][1] < v:
                    best[s_] = (s_, v, d_)
            return list(best.values())

        class _F:
            pass

        def ar_reset(limit=None):
            fb = _F()
            fb.w = None
            fb.r = list(ar["fence"])
            ar["fence"] = all_tickets([fb] + ar["bufs"] + B_mg)
            ar["bufs"] = []
            ar["off"] = 0
            ar["limit"] = MG_OFF if limit is None else limit

        def ar_alloc(name, shape, dt, nbuf=1):
            n = int(np.prod(shape[1:]))
            words = n if dt == F32 else (n + 1) // 2
            o = ar["off"]
            assert o + words <= ar["limit"], ("arena overflow", name, o, words, ar["limit"])
            ar["off"] = o + words
            ap = arena[:, o:o + words]
            if dt != F32:
                ap = ap.bitcast(BF16)
                if n % 2:
                    ap = ap[:, 0:n]
            if len(shape) == 3:
                ap = ap.rearrange("p (a b) -> p a b", a=shape[1])
            elif len(shape) == 4:
                ap = ap.rearrange("p (a b c) -> p a b c", a=shape[1], b=shape[2])
            bufs = []
            for i in range(nbuf):
                b = Buf(name + str(i))
                b.r = list(ar["fence"])
                ar["bufs"].append(b)
                bufs.append(b)
            return ap, bufs

        ps_all = stack.enter_context(nc.psum_tensor("ps_all", [128, 8, 512], F32))
        banks = [ps_all[:, i, :] for i in range(8)]
        B_bank = [Buf("bank%d" % i) for i in range(8)]
        rr = {"i": 0}

        def nbank(pool=(0, 1, 2, 3, 4, 5, 6, 7)):
            i = pool[rr["i"] % len(pool)]
            rr["i"] += 1
            return i

        slot_i = {"i": 0}

        def load_w(src, KC, C):
            si = slot_i["i"] % NSLOT
            slot_i["i"] += 1
            view = wsl[si][:, 0:KC * C].rearrange("p (k c) -> p k c", k=KC)
            pl.dma("pool", view, src.rearrange("(k p) c -> p k c", p=128), [], [B_wsl[si]], B_wsl[si])
            return view, B_wsl[si]

        def mm(out_ap, lhsT, rhs, start, stop, reads, bank, **kw):
            pl.op("pe", lambda e: e.matmul(out_ap, lhsT, rhs, start=start, stop=stop, **kw), reads, [B_bank[bank]])

        def act(out_ap, in_ap, func, reads, writes, **kw):
            pl.op("act", lambda e: e.activation(out=out_ap, in_=in_ap, func=func, **kw), reads, writes)

        def tt_(eng, out_ap, in0, in1, op, reads, writes):
            pl.op(eng, lambda e: e.tensor_tensor(out=out_ap, in0=in0, in1=in1, op=op), reads, writes)

        def ts_(eng, out_ap, in0, s1, s2, op0, op1, reads, writes):
            if s2 is None:
                pl.op(eng, lambda e: e.tensor_scalar(out=out_ap, in0=in0, scalar1=s1, scalar2=None, op0=op0), reads, writes)
            else:
                pl.op(eng, lambda e: e.tensor_scalar(out=out_ap, in0=in0, scalar1=s1, scalar2=s2, op0=op0, op1=op1), reads, writes)

        def stt_(eng, out_ap, in0, scalar, in1, op0, op1, reads, writes):
            pl.op(eng, lambda e: e.scalar_tensor_tensor(out=out_ap, in0=in0, scalar=scalar, in1=in1, op0=op0, op1=op1), reads, writes)

        def cp_(eng, out_ap, in_ap, reads, writes):
            pl.op(eng, lambda e: e.tensor_copy(out=out_ap, in_=in_ap), reads, writes)

        def mset(eng, ap, val, writes):
            pl.op(eng, lambda e: e.memset(ap, val), [], writes)

        def transpose(out_ap, in_ap, reads, bank):
            pl.op("pe", lambda e: e.transpose(out_ap, in_ap, ident_b[:]), list(reads) + [B_const], [B_bank[bank]])

        def proj_fm(wsrc, KC, ncols, rhs_fn, rhs_reads, evac, mm_pool=(0, 1, 2, 3), joint=False):
            wv, wb = load_w(wsrc, KC, ncols)
            done = []
            for m in range(ncols // 128):
                bk = nbank(mm_pool)
                for kc in range(KC):
                    mm(banks[bk][:, :], wv[:, kc, m * 128:(m + 1) * 128], rhs_fn(kc), kc == 0, kc == KC - 1,
                       [wb] + list(rhs_reads), bk)
                if joint:
                    done.append((m, bk))
                else:
                    evac(m, bk)
            if joint:
                evac(done)

        def tap(name, ap_sb, buf, shape):
            if name in taps:
                if name not in tapd:
                    tapd[name] = (dram("tap_" + name, shape, ap_sb.dtype, kind="ExternalOutput"), Buf("tap_" + name))
                pl.dma("sp", tapd[name][0], ap_sb, [buf], [tapd[name][1]], buf)

        B_xd = {}

        def xbuf(which, r0):
            return B_xd.setdefault((which, r0), Buf("xd"))

        outs_final = []

        for l in range(NL):
            src_x, src_id = (x_in, "in") if l == 0 else (xscr[(l - 1) % 2], "s%d" % ((l - 1) % 2))
            last = (l == NL - 1)
            dst_x, dst_id = (y_out, "out") if last else (xscr[l % 2], "s%d" % (l % 2))
            for (t, srcp) in ((normw, p_normw[l]), (convb, p_convb[l]), (dtb, p_dtb[l]), (a_neg, p_alog[l]),
                              (dskip, p_dskip[l]), (ssmnw, p_ssmnw[l]), (pscale, p_pscale[l])):
                pl.dma("sp", t[:], srcp, [], [B_par], B_par)
            pl.dma("sp", convw[:].rearrange("p a b -> p (a b)"), p_convw[l], [], [B_par], B_par)
            act(a_neg[:], a_neg[:], AF.Exp, [B_par], [B_par])
            ts_("dve", a_neg[:], a_neg[:], -1.0, None, ALU.mult, None, [B_par], [B_par])
            pl.op("dve", lambda e: e.reciprocal(out=inv_nw[:], in_=ssmnw[:]), [B_par], [B_par])

            prefetched = {"v": False}
            for sq in range(NSEQ):
                mset("dve", ST[:], 0.0, B_ST)
                mset("dve", STb[:], 0.0, B_STb)
                mset("dve", chalo[:], 0.0, B_chalo)
                for tt in range(NTT):
                    tok0 = sq * S + tt * TT
                    first_tile = (tt == 0)
                    def norm_block(tokb, j, xblk_a, B_xblk, hb_a, B_hb, tpool=(6, 7)):
                        r0 = tokb + j * 128
                        xb, Bx = xblk_a[:, j % 2, :], B_xblk[j % 2]
                        hbj, Bh = hb_a[:, j % 2, :], B_hb[j % 2]
                        ssj, Bss = sst[:, (j % 2) * 4:(j % 2) * 4 + 1], B_sst[j % 2]
                        rsj = sst[:, (j % 2) * 4 + 1:(j % 2) * 4 + 2]
                        pl.dma("sp", xb, src_x[r0:r0 + 128, :], [xbuf(src_id, r0)], [Bx], Bx)
                        mset("dve", ssj, 0.0, [Bss])
                        act(hbj, xb, AF.Square, [Bx, Bss], [Bh, Bss], accum_out=ssj)
                        act(rsj, ssj, AF.Ln, [Bss], [Bss], scale=1.0 / D, bias=EPS)
                        act(rsj, rsj, AF.Exp, [Bss], [Bss], scale=-0.5)
                        stt_("dve", hbj, xb, rsj, normw[:], ALU.mult, ALU.mult, [Bx, Bss, B_par], [Bh])
                        bk = nbank(tpool)
                        pbf = banks[bk][:, :].bitcast(BF16)
                        for kc in range(8):
                            transpose(pbf[:, kc * 128:(kc + 1) * 128], hbj[:, kc * 128:(kc + 1) * 128], [Bh], bk)
                        cp_("dve", hT[:, :, j * 128:(j + 1) * 128], pbf.rearrange("p (k t) -> p k t", k=8),
                            [B_bank[bk]], [B_hT])

                    if not prefetched["v"]:
                        ar_reset()
                        xblk_a, B_xblk = ar_alloc("xblk", [128, 2, D], F32, 2)
                        hb_a, B_hb = ar_alloc("hb", [128, 2, D], BF16, 2)
                        for j in range(4):
                            norm_block(tok0, j, xblk_a, B_xblk, hb_a, B_hb)
                    prefetched["v"] = False
                    tap("hT", hT[:].rearrange("p k t -> p (k t)"), B_hT, [128, 8 * TT])

                    rhs_h = lambda kc: hT[:, kc, :]

                    def gate_block(mb, gi):
                        if mb % 2 == 0:
                            c0 = C_MG + gi * D + mb * 128

                            def ev(m, bk):
                                act(gate[m][:], banks[bk][:, :], AF.Sigmoid, [B_bank[bk]], [B_gate[m]])
                            proj_fm(w_in[l][:, c0:c0 + 256], 8, 256, rhs_h, [B_hT], ev)
                        return gate[mb % 2], B_gate[mb % 2]

                    ar_reset(ARENA_F)
                    zs, B_zs = ar_alloc("zs", [128, 16, TT], BF16, 16)
                    rstdb, B_rstdb = ar_alloc("rstdb", [128, TT], F32, 1)
                    xbcT, B_xbc = ar_alloc("xbcT", [128, 20, TT], BF16, 20)
                    upre, B_upre = ar_alloc("upre", [128, 2, TT + 4], F32, 2)
                    cacc, B_cacc = ar_alloc("cacc", [128, 2, TT], F32, 2)
                    dts, B_dts = ar_alloc("dts", [128, 8, 128], F32, 1)
                    B_dts = B_dts[0]
                    xdt, B_xdt = ar_alloc("xdt", [128, 2, 2048], BF16, 2)
                    xdtd, B_xdtd = ar_alloc("xdtd", [128, 2, 2048], BF16, 2)
                    Btok, B_Btok = ar_alloc("Btok", [128, 2, 256], BF16, 2)
                    cbTm, B_cbTm = ar_alloc("cbTm", [128, 2, 256], BF16, 2)
                    rseg, B_rseg = ar_alloc("rseg", [128, 2, 1024], F32, 2)
                    MT, B_MT = ar_alloc("MT", [128, 16, 128], BF16, 2)
                    ytmp, B_ytmp = ar_alloc("ytmp", [128, 2, 512], F32, 2)
                    ytok, B_ytok = ar_alloc("ytok", [128, 2, 2048], BF16, 8)
                    tA, B_tA = ar_alloc("tA", [128, 2, 128], F32, 2)
                    sq_, B_sq = ar_alloc("sq", [128, 2, 128], BF16, 2)
                    B_rstdb = B_rstdb[0]

                    wv, wb = load_w(w_in[l][:, C_DT:C_DT + 32], 8, 32)
                    bk = nbank((0, 1, 2, 3))
                    for j in range(4):
                        for kc in range(8):
                            mm(banks[bk][:, j * 32:(j + 1) * 32], hT[:, kc, j * 128:(j + 1) * 128], wv[:, kc, :],
                               kc == 0, kc == 7, [wb, B_hT], bk)
                    d3 = lambda i: dts[:, i, :].rearrange("p (j h) -> p j h", j=4)
                    bc4 = lambda t: t[:].unsqueeze(1).to_broadcast([128, 4, 32])
                    tt_("dve", d3(0), banks[bk][:, 0:128].rearrange("p (j h) -> p j h", j=4), bc4(dtb), ALU.add,
                        [B_bank[bk], B_par], [B_dts])
                    act(dts[:, 0, :], dts[:, 0, :], AF.Exp, [B_dts], [B_dts])
                    act(dts[:, 1, :], dts[:, 0, :], AF.Ln, [B_dts], [B_dts], bias=1.0)
                    tt_("dve", d3(2), d3(1), bc4(a_neg), ALU.mult, [B_dts, B_par], [B_dts])
                    bk = nbank((0, 1, 2, 3))
                    mm(banks[bk][:, 0:128], U_f[:], dts[:, 2, :], True, True, [B_dts, B_const], bk)
                    mm(banks[bk][:, 128:256], ones_f[:], dts[:, 2, :], True, True, [B_dts, B_const], bk)
                    cp_("dve", dts[:, 3, :], banks[bk][:, 0:128], [B_bank[bk]], [B_dts])
                    tt_("dve", dts[:, 4, :], banks[bk][:, 128:256], dts[:, 3, :], ALU.subtract, [B_bank[bk], B_dts], [B_dts])
                    act(dts[:, 4, :], dts[:, 4, :], AF.Exp, [B_dts], [B_dts])
                    act(dts[:, 5, :], dts[:, 3, :], AF.Exp, [B_dts], [B_dts])
                    act(dts[:, 6, :], banks[bk][:, 128:256], AF.Exp, [B_bank[bk]], [B_dts])
                    tt_("dve", dts[:, 7, :], dts[:, 1, :], dts[:, 4, :], ALU.mult, [B_dts], [B_dts])
                    tap("dts", dts[:].rearrange("p k t -> p (k t)"), B_dts, [128, 1024])
                    def z_blk(blk):
                        def ev(m, bk, blk=blk):
                            cb = blk * 2 + m
                            act(zs[:, cb, :], banks[bk][:, :], AF.Silu, [B_bank[bk]], [B_zs[cb]])
                        proj_fm(w_in[l][:, C_Z + blk * 256:C_Z + (blk + 1) * 256], 8, 256, rhs_h, [B_hT], ev)
                    def x_blk(blk):
                        def ev(done, blk=blk):
                            cbs = [(blk * 2 + m, bk) for (m, bk) in done]
                            for cb, bk in cbs:
                                u, Bu = upre[:, cb % 2, :], B_upre[cb % 2]
                                ca, Bca = cacc[:, cb % 2, :], B_cacc[cb % 2]
                                act(u[:, 4:TT + 4], banks[bk][:, :], AF.Copy, [B_bank[bk]], [Bu])
                                act(ca, banks[bk][:, :], AF.Identity, [B_bank[bk], B_par], [Bca],
                                    scale=convw[:, cb, 3:4], bias=convb[:, cb:cb + 1])
                                cp_("dve", u[:, 1:4], chalo[:, cb, :], [B_chalo[cb]], [Bu])
                            for k in (2, 1, 0):
                                for cb, bk in cbs:
                                    u, Bu = upre[:, cb % 2, :], B_upre[cb % 2]
                                    ca, Bca = cacc[:, cb % 2, :], B_cacc[cb % 2]
                                    stt_("dve", ca, u[:, 1 + k:1 + k + TT], convw[:, cb, k:k + 1], ca, ALU.mult, ALU.add,
                                         [Bu, B_par, Bca], [Bca])
                            for cb, bk in cbs:
                                u, Bu = upre[:, cb % 2, :], B_upre[cb % 2]
                                ca, Bca = cacc[:, cb % 2, :], B_cacc[cb % 2]
                                cp_("dve", chalo[:, cb, :], u[:, TT + 1:TT + 4], [Bu], [B_chalo[cb]])
                                act(xbcT[:, cb, :], ca, AF.Silu, [Bca], [B_xbc[cb]])
                        proj_fm(w_in[l][:, C_XBC + blk * 256:C_XBC + (blk + 1) * 256], 8, 256, rhs_h, [B_hT], ev, joint=True)
                    for blk in range(10):
                        x_blk(blk)
                        if blk < 8:
                            z_blk(blk)
                    tap("zs", zs[:].rearrange("p k t -> p (k t)"), B_zs[15], [128, 16 * TT])
                    tap("xbcT", xbcT[:].rearrange("p k t -> p (k t)"), B_xbc[19], [128, 20 * TT])
                    ssbk = nbank((4, 5))

                    def hsl(c, i, h0, n):
                        return dts[:, i, c * 32 + h0:c * 32 + h0 + n]

                    def ssd_F_thunks(c):
                        p = c % 2
                        csl = slice(c * 128, (c + 1) * 128)

                        def xs_half(hh):
                            bk = nbank((6, 7))
                            pbf = banks[bk][:, :].bitcast(BF16)
                            for i in range(8):
                                cb = hh * 8 + i
                                transpose(pbf[:, i * 128:(i + 1) * 128], xbcT[:, cb, csl], [B_xbc[cb]], bk)
                            pv = pbf.rearrange("p (h d) -> p h d", d=64)
                            tt_("dve", xdt[:, p, hh * 1024:(hh + 1) * 1024].rearrange("p (h d) -> p h d", d=64), pv,
                                hsl(c, 1, hh * 16, 16).unsqueeze(2).to_broadcast([128, 16, 64]), ALU.mult,
                                [B_bank[bk], B_dts], [B_xdt[p]])
                            tt_("dve", xdtd[:, p, hh * 1024:(hh + 1) * 1024].rearrange("p (h d) -> p h d", d=64), pv,
                                hsl(c, 7, hh * 16, 16).unsqueeze(2).to_broadcast([128, 16, 64]), ALU.mult,
                                [B_bank[bk], B_dts], [B_xdtd[p]])

                        def bcb():
                            bk = nbank((6, 7))
                            pbf = banks[bk][:, :].bitcast(BF16)
                            for g in range(2):
                                transpose(pbf[:, g * 128:(g + 1) * 128], xbcT[:, 16 + g, csl], [B_xbc[16 + g]], bk)
                            cp_("dve", Btok[:, p, :], pbf[:, 0:256], [B_bank[bk]], [B_Btok[p]])
                            bk = nbank((0, 1, 2, 3))
                            for g in range(2):
                                mm(banks[bk][:, g * 128:(g + 1) * 128], xbcT[:, 16 + g, csl], xbcT[:, 18 + g, csl], True, True,
                                   [B_xbc[16 + g], B_xbc[18 + g]], bk)
                            tt_("dve", cbTm[:, p, :].rearrange("p (g l) -> p g l", g=2),
                                banks[bk][:, 0:256].rearrange("p (g l) -> p g l", g=2),
                                U_f[:].unsqueeze(1).to_broadcast([128, 2, 128]), ALU.mult, [B_bank[bk], B_const], [B_cbTm[p]])

                        return [lambda: xs_half(0), lambda: xs_half(1), bcb]

                    def ssd_M(c, hook=None):
                        p = c % 2
                        csl = slice(c * 128, (c + 1) * 128)
                        st = {}

                        def m0(hq):
                            rs = rseg[:, hq % 2, :].rearrange("p (h l) -> p h l", h=8)
                            tt_("pool", rs, U_f[:].unsqueeze(1).to_broadcast([128, 8, 128]),
                                hsl(c, 2, hq * 8, 8).unsqueeze(2).to_broadcast([128, 8, 128]), ALU.mult,
                                [B_const, B_dts], [B_rseg[hq % 2]])

                        def m1(hq):
                            mo = (hq % 2) * 8
                            for q in range(2):
                                bk = nbank((0, 1, 2, 3))
                                mm(banks[bk][:, :], Ls_f[:], rseg[:, hq % 2, q * 512:(q + 1) * 512], True, True,
                                   [B_rseg[hq % 2], B_const], bk)
                                act(MT[:, mo + q * 4:mo + q * 4 + 4, :].rearrange("p h l -> p (h l)"),
                                    banks[bk][:, :], AF.Exp, [B_bank[bk]], [B_MT[hq % 2]])

                        def m2(hq):
                            mo = (hq % 2) * 8
                            g = hq // 2
                            tt_("dve", MT[:, mo:mo + 8, :], MT[:, mo:mo + 8, :],
                                cbTm[:, p, g * 128:(g + 1) * 128].unsqueeze(1).to_broadcast([128, 8, 128]), ALU.mult,
                                [B_MT[hq % 2], B_cbTm[p]], [B_MT[hq % 2]])

                        def m3(hq):
                            mo = (hq % 2) * 8
                            g = hq // 2
                            ybk = nbank((0, 1, 2, 3))
                            for i in range(8):
                                h = hq * 8 + i
                                mm(banks[ybk][:, i * 64:(i + 1) * 64], MT[:, mo + i, :], xdt[:, p, h * 64:(h + 1) * 64], True, True,
                                   [B_MT[hq % 2], B_xdt[p]], ybk)
                            obk = nbank((0, 1, 2, 3))
                            mm(banks[obk][:, :], xbcT[:, 18 + g, csl], STb[:, hq * 512:(hq + 1) * 512], True, True,
                               [B_xbc[18 + g], B_STb[hq]], obk)
                            yt, Byt = ytmp[:, hq % 2, :], B_ytmp[hq % 2]
                            tt_("dve", yt.rearrange("p (h d) -> p h d", d=64),
                                banks[obk][:, :].rearrange("p (h d) -> p h d", d=64),
                                hsl(c, 5, hq * 8, 8).unsqueeze(2).to_broadcast([128, 8, 64]), ALU.mult,
                                [B_bank[obk], B_dts], [Byt])
                            tt_("dve", ytok[:, p, hq * 512:(hq + 1) * 512], banks[ybk][:, :], yt, ALU.add,
                                [B_bank[ybk], Byt], [B_ytok[p * 4 + hq]])

                        pipeline(list(range(4)), [m0, m1, m2, m3], hook=hook)

                    def ssd_S(c):
                        p = c % 2
                        for hq in range(4):
                            g = hq // 2
                            bk = nbank((0, 1, 2, 3))
                            mm(banks[bk][:, :], Btok[:, p, g * 128:(g + 1) * 128], xdtd[:, p, hq * 512:(hq + 1) * 512], True, True,
                               [B_Btok[p], B_xdtd[p]], bk)
                            stv = ST[:, hq * 512:(hq + 1) * 512]
                            tt_("pool", stv.rearrange("p (h d) -> p h d", d=64), stv.rearrange("p (h d) -> p h d", d=64),
                                hsl(c, 6, hq * 8, 8).unsqueeze(2).to_broadcast([128, 8, 64]), ALU.mult, [B_ST[hq], B_dts], [B_ST[hq]])
                            tt_("dve", stv, stv, banks[bk][:, :], ALU.add, [B_ST[hq], B_bank[bk]], [B_ST[hq]])
                            act(STb[:, hq * 512:(hq + 1) * 512], stv, AF.Copy, [B_ST[hq]], [B_STb[hq]])

                    def ssd_T_thunks(c):
                        p = c % 2
                        csl = slice(c * 128, (c + 1) * 128)
                        th = []
                        for hh in range(2):
                            stt = {}

                            def tr(hh=hh, stt=stt):
                                bk = nbank((6, 7))
                                stt["bk"] = bk
                                pbf = banks[bk][:, :].bitcast(BF16)
                                for i in range(8):
                                    cb = hh * 8 + i
                                    transpose(pbf[:, i * 128:(i + 1) * 128], ytok[:, p, cb * 128:(cb + 1) * 128], [B_ytok[p * 4 + cb // 4]], bk)
                            th.append(tr)
                            for i2 in range(4):
                                def grp(hh=hh, i2=i2, stt=stt):
                                    bk = stt["bk"]
                                    pbf = banks[bk][:, :].bitcast(BF16)
                                    pair = [(hh * 8 + i2 * 2 + d_, i2 * 2 + d_) for d_ in range(2)]
                                    for cb, i in pair:
                                        stt_("dve", tA[:, cb % 2, :], xbcT[:, cb, csl], dskip[:, cb:cb + 1], pbf[:, i * 128:(i + 1) * 128],
                                             ALU.mult, ALU.add, [B_xbc[cb], B_par, B_bank[bk]], [B_tA[cb % 2]])
                                    for cb, i in pair:
                                        stt_("dve", zs[:, cb, csl], tA[:, cb % 2, :], ssmnw[:, cb:cb + 1], zs[:, cb, csl], ALU.mult, ALU.mult,
                                             [B_tA[cb % 2], B_par, B_zs[cb]], [B_zs[cb]])
                                    for cb, i in pair:
                                        act(sq_[:, cb % 2, :], zs[:, cb, csl], AF.Square, [B_zs[cb], B_par], [B_sq[cb % 2]],
                                            scale=inv_nw[:, cb:cb + 1])
                                    for cb, i in pair:
                                        mm(banks[ssbk][:, csl], ones_b[:], sq_[:, cb % 2, :], cb == 0, cb == 15, [B_sq[cb % 2], B_const], ssbk)
                                th.append(grp)
                        return th

                    def kv_thunks(blk):
                        def kpart():
                            def ev(m, bk, blk=blk):
                                cb = blk * 2 + m
                                cp_("dve", Kc[:, cb, tt * TT:(tt + 1) * TT], banks[bk][:, :], [B_bank[bk]], [B_Kc[tt]])
                            proj_fm(w_in[l][:, C_K + blk * 256:C_K + (blk + 1) * 256], 8, 256, rhs_h, [B_hT], ev)

                        def vpart():
                            wv, wb = load_w(w_in[l][:, C_V + blk * 256:C_V + (blk + 1) * 256], 8, 256)
                            for jp in range(2):
                                bk = nbank((0, 1, 2, 3))
                                for jj in range(2):
                                    j = jp * 2 + jj
                                    for kc in range(8):
                                        mm(banks[bk][:, jj * 256:(jj + 1) * 256], hT[:, kc, j * 128:(j + 1) * 128], wv[:, kc, :],
                                           kc == 0, kc == 7, [wb, B_hT], bk)
                                cp_("dve", Vc[:, tt * 4 + jp * 2:tt * 4 + jp * 2 + 2, blk * 256:(blk + 1) * 256],
                                    banks[bk][:, :].rearrange("p (j c) -> p j c", j=2), [B_bank[bk]], [B_Vc[tt]])
                        return [kpart, vpart]

                    for th_ in ssd_F_thunks(0):
                        th_()
                    for c in range(4):
                        fill = []
                        if c >= 1:
                            fill += ssd_T_thunks(c - 1)
                        if c + 1 < 4:
                            fill += ssd_F_thunks(c + 1)
                        fill += kv_thunks(c)
                        nst = 7
                        per = (len(fill) + nst - 1) // nst

                        def hook(step, fill=fill, per=per):
                            for _ in range(per):
                                if fill:
                                    fill.pop(0)()
                        ssd_M(c, hook)
                        ssd_S(c)
                        while fill:
                            fill.pop(0)()
                    for th_ in ssd_T_thunks(3):
                        th_()
                    act(rstdb[:], banks[ssbk][:, :], AF.Ln, [B_bank[ssbk]], [B_rstdb], scale=1.0 / SSMW, bias=EPS)
                    act(rstdb[:], rstdb[:], AF.Exp, [B_rstdb], [B_rstdb], scale=-0.5)
                    tap("ygn", zs[:].rearrange("p k t -> p (k t)"), B_zs[15], [128, 16 * TT])
                    tap("rstdb", rstdb[:], B_rstdb, [128, TT])
                    fz = all_tickets(ar["bufs"])
                    for b_ in B_mg:
                        b_.r = list(fz) + list(b_.r)
                    for mb in range(8):
                        gbuf, Bg = gate_block(mb, 0)

                        def ev(m, bk, mb=mb, gbuf=gbuf, Bg=Bg):
                            tt_("dve", t1[mb % 2][:], banks[bk][:, :], rstdb[:], ALU.mult, [B_bank[bk], B_rstdb], [B_t1[mb % 2]])
                            tt_("dve", mg[:, mb, :], t1[mb % 2][:], gbuf[:], ALU.mult, [B_t1[mb % 2], Bg], [B_mg[mb]])
                        proj_fm(w_ps[l][:, mb * 128:(mb + 1) * 128], 16, 128, lambda kc: zs[:, kc, :], B_zs, ev)
                    tap("mg0", mg[:].rearrange("p k t -> p (k t)"), B_mg[7], [128, 8 * TT])

                    ar_reset()
                    ub, B_ub = ar_alloc("ub", [128, 8, TT + 16], F32, 8)
                    pa, B_pa = ar_alloc("pa", [128, 2, TT + 16], F32, 2)
                    pb_, B_pb = ar_alloc("pb", [128, 2, TT + 16], F32, 2)
                    mixed, B_mixed = ar_alloc("mixed", [128, 8, TT], BF16, 8)
                    sgp, B_sgp = ar_alloc("sgp", [128, 8, TT], BF16, 8)
                    pooled, B_pooled = ar_alloc("pooled", [128, 8, TT], BF16, 8)
                    for blk in range(4):
                        def ev(m, bk, blk=blk):
                            cb = blk * 2 + m
                            act(ub[:, cb, 16:TT + 16], banks[bk][:, :], AF.Copy, [B_bank[bk]], [B_ub[cb]])
                        proj_fm(w_in[l][:, C_PU + blk * 256:C_PU + (blk + 1) * 256], 8, 256, rhs_h, [B_hT], ev)
                    for blk in range(4):
                        def ev(m, bk, blk=blk):
                            cb = blk * 2 + m
                            act(sgp[:, cb, :], banks[bk][:, :], AF.Silu, [B_bank[bk]], [B_sgp[cb]])
                        proj_fm(w_in[l][:, C_PG + blk * 256:C_PG + (blk + 1) * 256], 8, 256, rhs_h, [B_hT], ev)
                    for cb in range(8):
                        g = cb // 2
                        if first_tile:
                            mset("dve", ub[:, cb, 0:16], 0.0, [B_ub[cb]])
                        else:
                            cp_("dve", ub[:, cb, 0:16], phalo[:, cb, :], [B_phalo[cb]], [B_ub[cb]])
                        W_ = TT + 16
                        cur, Bcur = ub[:, cb, :], B_ub[cb]
                        tmp = [(pa[:, cb % 2, :], B_pa[cb % 2]), (pb_[:, cb % 2, :], B_pb[cb % 2])]
                        for lev in range(g + 1):
                            sh = 1 << lev
                            dst, Bdst = tmp[lev % 2]
                            tt_("dve", dst[:, sh:W_], cur[:, sh:W_], cur[:, 0:W_ - sh], ALU.add, [Bcur], [Bdst])
                            if lev > 0:
                                pass
                            cur, Bcur = dst, Bdst
                        w = 2 << g
                        stt_("dve", mixed[:, cb, :], cur[:, 16:W_], 1.0 / w, ub[:, cb, 16:W_], ALU.mult, ALU.subtract,
                             [Bcur, B_ub[cb]], [B_mixed[cb]])
                        if first_tile:
                            tt_("dve", cur[:, 16:32], cur[:, 16:32], invcnt[:, g, :], ALU.mult, [Bcur, B_const], [Bcur])
                            tt_("dve", mixed[:, cb, 0:16], cur[:, 16:32], ub[:, cb, 16:32], ALU.subtract,
                                [Bcur, B_ub[cb]], [B_mixed[cb]])
                        cp_("dve", phalo[:, cb, :], ub[:, cb, TT:TT + 16], [B_ub[cb]], [B_phalo[cb]])
                    wv_all, wb = load_w(pool_w[l].rearrange("g r c -> (g r) c"), 8, 256)
                    for g in range(4):
                        wv = wv_all[:, 2 * g:2 * g + 2, :]
                        for m in range(2):
                            bk = nbank((0, 1, 2, 3))
                            for kc in range(2):
                                mm(banks[bk][:, :], wv[:, kc, m * 128:(m + 1) * 128], mixed[:, 2 * g + kc, :], kc == 0, kc == 1,
                                   [wb, B_mixed[2 * g + kc]], bk)
                            cb = 2 * g + m
                            stt_("dve", pooled[:, cb, :], banks[bk][:, :], pscale[:, cb:cb + 1], sgp[:, cb, :], ALU.mult, ALU.mult,
                                 [B_bank[bk], B_par, B_sgp[cb]], [B_pooled[cb]])
                    tap("pooled", pooled[:].rearrange("p k t -> p (k t)"), B_pooled[7], [128, 8 * TT])
                    for mb0 in range(0, 8, 2):
                        gate_block(mb0, 1)

                        def ev(m, bk, mb0=mb0):
                            mb = mb0 + m
                            tt_("dve", t1[m][:], banks[bk][:, :], gate[m][:], ALU.mult, [B_bank[bk], B_gate[m]], [B_t1[m]])
                            tt_("dve", mg[:, mb, :], mg[:, mb, :], t1[m][:], ALU.add, [B_t1[m], B_mg[mb]], [B_mg[mb]])
                        proj_fm(w_pp[l][:, mb0 * 128:(mb0 + 2) * 128], 8, 256, lambda kc: pooled[:, kc, :], B_pooled, ev)

                    ar_reset()
                    qT, B_qT = ar_alloc("qT", [128, 8, TT], BF16, 8)
                    sgb, B_sgb = ar_alloc("sgb", [128, 8, TT], BF16, 8)
                    sbo, B_sbo = ar_alloc("sbo", [128, 8, TT], BF16, 8)
                    NE = 4
                    e_sb, B_e = ar_alloc("e_sb", [128, NE, TT], F32, NE)
                    sp_sb, B_sp = ar_alloc("sp_sb", [128, 4, TT], BF16, 4)
                    tmp_sb, B_tmp = ar_alloc("tmp_sb", [128, NE, TT], F32, NE)
                    att_sb, B_att = ar_alloc("att_sb", [128, NE, TT], BF16, NE)
                    carry, B_carry = ar_alloc("carry", [128, 4, TT], F32, 4)
                    g2all, B_g2 = ar_alloc("g2all", [128, 8, TT], BF16, 8)
                    has_next = not (sq == NSEQ - 1 and tt == NTT - 1)
                    if has_next:
                        nxblk, B_nxblk = ar_alloc("nxblk", [128, 2, D], F32, 2)
                        nhb, B_nhb = ar_alloc("nhb", [128, 2, D], BF16, 2)
                        ntok0 = tok0 + TT
                    for blk in range(4):
                        def ev(m, bk, blk=blk):
                            cb = blk * 2 + m
                            pl.op("act", lambda e, o=qT[:, cb, :], i_=banks[bk][:, :]: e.mul(o, i_, 0.125), [B_bank[bk]], [B_qT[cb]])
                        proj_fm(w_in[l][:, C_Q + blk * 256:C_Q + (blk + 1) * 256], 8, 256, rhs_h, [B_hT], ev)
                    for blk in range(4):
                        def ev(m, bk, blk=blk):
                            cb = blk * 2 + m
                            act(sgb[:, cb, :], banks[bk][:, :], AF.Silu, [B_bank[bk]], [B_sgb[cb]])
                        proj_fm(w_in[l][:, C_SG + blk * 256:C_SG + (blk + 1) * 256], 8, 256, rhs_h, [B_hT], ev)
                    nkt = (tt + 1) * 4
                    items = []
                    for hp in range(8):
                        for a in reversed(range(nkt)):
                            items.append({"hp": hp, "a": a, "i": len(items)})
                    for it in items:
                        hp, a, i = it["hp"], it["a"], it["i"]
                        r = a - tt * 4
                        it["c0"] = max(r, 0) * 128
                        it["diag"] = r >= 0
                        it["first"] = (a == nkt - 1)
                        it["last"] = (a == 0)
                        it["zb"] = [(i % 2) * 2, (i % 2) * 2 + 1]
                        it["tb"] = [4, 5]
                        it["ob"] = 6 + (hp % 2)
                        it["e"] = [(i % 2) * 2, (i % 2) * 2 + 1]
                        it["sp"] = [(i % 2) * 2, (i % 2) * 2 + 1]
                        it["cy"] = [(hp % 2) * 2, (hp % 2) * 2 + 1]
                        it["kreads"] = [B_Kc[a // 4]]
                        it["vreads"] = [B_Vc[a // 4]]

                    def s0(it):
                        c0, hp, a = it["c0"], it["hp"], it["a"]
                        for hh in range(2):
                            zb, po = it["zb"][hh], 64 * hh
                            mm(banks[zb][:, c0:TT], Kc[po:po + 64, hp, a * 128:(a + 1) * 128], qT[po:po + 64, hp, c0:TT],
                               True, not it["diag"], it["kreads"] + [B_qT[hp]], zb)
                        if it["diag"]:
                            for hh in range(2):
                                zb = it["zb"][hh]
                                mm(banks[zb][:, c0:c0 + 128], ident_b[:], NEG_b[:], False, True, [B_const], zb, skip_group_check=True)

                    def s1(it):
                        c0 = it["c0"]
                        z0, e0, p0 = it["zb"][0], it["e"][0], it["sp"][0]
                        act(e_sb[:, e0:e0 + 2, c0:TT], ps_all[:, z0:z0 + 2, c0:TT], AF.Exp, [B_bank[z0], B_bank[z0 + 1]],
                            [B_e[e0], B_e[e0 + 1]])
                        act(sp_sb[:, p0:p0 + 2, c0:TT], e_sb[:, e0:e0 + 2, c0:TT], AF.Ln, [B_e[e0], B_e[e0 + 1]],
                            [B_sp[p0], B_sp[p0 + 1]], bias=1.0)

                    def s2(it):
                        c0 = it["c0"]
                        for hh in range(2):
                            zb, si = it["zb"][hh], it["sp"][hh]
                            mm(banks[zb][:, c0:TT], nLi_b[:], sp_sb[:, si, c0:TT], False, True, [B_sp[si], B_const], zb,
                               skip_group_check=True)
                        if not it["last"]:
                            for hh in range(2):
                                tb, si = it["tb"][hh], it["sp"][hh]
                                mm(banks[tb][:, c0:TT], nOnes_b[:], sp_sb[:, si, c0:TT], True, True, [B_sp[si], B_const], tb)

                    def s3(it):
                        c0 = it["c0"]
                        z0, e0, cy0 = it["zb"][0], it["e"][0], it["cy"][0]
                        Bcy = [B_carry[cy0], B_carry[cy0 + 1]]
                        if it["first"]:
                            mset("dve", carry[:, cy0:cy0 + 2, :], 0.0, Bcy)
                        tt_("dve", tmp_sb[:, e0:e0 + 2, c0:TT], ps_all[:, z0:z0 + 2, c0:TT], carry[:, cy0:cy0 + 2, c0:TT], ALU.add,
                            [B_bank[z0], B_bank[z0 + 1]] + Bcy, [B_tmp[e0], B_tmp[e0 + 1]])
                        if not it["last"]:
                            tt_("dve", carry[:, cy0:cy0 + 2, c0:TT], ps_all[:, 4:6, c0:TT], carry[:, cy0:cy0 + 2, c0:TT], ALU.add,
                                [B_bank[4], B_bank[5]] + Bcy, Bcy)

                    def s4(it):
                        c0 = it["c0"]
                        e0 = it["e"][0]
                        act(att_sb[:, e0:e0 + 2, c0:TT], tmp_sb[:, e0:e0 + 2, c0:TT], AF.Exp, [B_tmp[e0], B_tmp[e0 + 1]],
                            [B_att[e0], B_att[e0 + 1]])

                    def s5(it):
                        c0, ob, a, hp = it["c0"], it["ob"], it["a"], it["hp"]
                        for hh in range(2):
                            ei, po, h = it["e"][hh], 64 * hh, 2 * hp + hh
                            mm(banks[ob][po:po + 64, c0:TT], Vc[:, a, h * 64:(h + 1) * 64], att_sb[:, ei, c0:TT],
                               it["first"], it["last"], it["vreads"] + [B_att[ei]], ob, skip_group_check=True)
                        if it["last"]:
                            tt_("dve", sbo[:, hp, :], banks[ob][:, :], sgb[:, hp, :], ALU.mult, [B_bank[ob], B_sgb[hp]], [B_sbo[hp]])

                    for mb in range(0, 8, 2):
                        c0g = C_MG + 2 * D + mb * 128

                        def evg(m, bk, mb=mb):
                            act(g2all[:, mb + m, :], banks[bk][:, :], AF.Sigmoid, [B_bank[bk]], [B_g2[mb + m]])
                        proj_fm(w_in[l][:, c0g:c0g + 256], 8, 256, rhs_h, [B_hT], evg)
                    hook = None
                    if has_next:
                        nsteps = len(items) + 4
                        at = {max(2, (nsteps * (jn + 1)) // 6): jn for jn in range(4)}

                        def hook(step):
                            if step in at:
                                norm_block(ntok0, at[step], nxblk, B_nxblk, nhb, B_nhb, tpool=(4, 5))
                    pipeline(items, [s0, s1, lambda it: (s2(it), s3(it)), s4, s5], order=[2, 4, 3, 1, 0], hook=hook)
                    if has_next:
                        prefetched["v"] = True
                    tap("sbo", sbo[:].rearrange("p k t -> p (k t)"), B_sbo[7], [128, 8 * TT])
                    for mb0 in range(0, 8, 2):
                        def ev(m, bk, mb0=mb0):
                            mb = mb0 + m
                            tt_("dve", t1[m][:], banks[bk][:, :], g2all[:, mb, :], ALU.mult, [B_bank[bk], B_g2[mb]], [B_t1[m]])
                            tt_("dve", mg[:, mb, :], mg[:, mb, :], t1[m][:], ALU.add, [B_t1[m], B_mg[mb]], [B_mg[mb]])
                        proj_fm(w_pb[l][:, mb0 * 128:(mb0 + 2) * 128], 8, 256, lambda kc: sbo[:, kc, :], B_sbo, ev)

                    ar_reset()
                    mgb, B_mgb = ar_alloc("mgb", [128, 8, TT], BF16, 8)
                    xo, B_xo = ar_alloc("xo", [128, 4, D], F32, 4)
                    if last and final:
                        junk, B_junk = ar_alloc("junk", [128, D], BF16, 1)
                        fnw, B_fnw = ar_alloc("fnw", [128, D], F32, 1)
                        pl.dma("sp", fnw, p_fnw, [], [B_fnw[0]], B_fnw[0])
                    for mb in range(8):
                        cp_("dve", mgb[:, mb, :], mg[:, mb, :], [B_mg[mb]], [B_mgb[mb]])
                    for j in range(4):
                        r0 = tok0 + j * 128
                        pl.dma("sp", xo[:, j, :], src_x[r0:r0 + 128, :], [xbuf(src_id, r0)], [B_xo[j]], B_xo[j])
                    for half in range(2):
                        bks = [nbank((0, 1, 2, 3)) for j in range(4)]
                        for kh in range(2):
                            wv, wb = load_w(w_out[l][kh * 512:(kh + 1) * 512, half * 512:(half + 1) * 512], 4, 512)
                            for j in range(4):
                                for k4 in range(4):
                                    kc = kh * 4 + k4
                                    mm(banks[bks[j]][:, :], mgb[:, kc, j * 128:(j + 1) * 128], wv[:, k4, :], kc == 0, kc == 7,
                                       [wb, B_mgb[kc]], bks[j])
                        for j in range(4):
                            xv = xo[:, j, half * 512:(half + 1) * 512]
                            tt_("dve", xv, banks[bks[j]][:, :], xv, ALU.add, [B_bank[bks[j]], B_xo[j]], [B_xo[j]])
                    for j in range(4):
                        r0 = tok0 + j * 128
                        xov, Bxo = xo[:, j, :], B_xo[j]
                        if last and final:
                            ssj, Bss = sst[:, (j % 2) * 4:(j % 2) * 4 + 1], B_sst[j % 2]
                            rsj = sst[:, (j % 2) * 4 + 1:(j % 2) * 4 + 2]
                            mset("dve", ssj, 0.0, [Bss])
                            act(junk, xov, AF.Square, [Bxo, Bss], [B_junk[0], Bss], accum_out=ssj)
                            act(rsj, ssj, AF.Ln, [Bss], [Bss], scale=1.0 / D, bias=EPS)
                            act(rsj, rsj, AF.Exp, [Bss], [Bss], scale=-0.5)
                            stt_("dve", xov, xov, rsj, fnw, ALU.mult, ALU.mult, [Bxo, Bss, B_fnw[0]], [Bxo])
                        pl.dma("sp", dst_x[r0:r0 + 128, :], xov, [Bxo], [xbuf(dst_id, r0)], Bxo)
                        if last:
                            outs_final.append(xbuf(dst_id, r0))
        pl.final_wait("sp", outs_final + [v[1] for v in tapd.values()])
        pl.emit()
    return nc, pl


def _prep_inputs(inp, NL, l0=0):
    f = lambda a: np.ascontiguousarray(np.asarray(a, dtype=np.float32))
    sl = slice(l0, l0 + NL)
    m = {}
    m["w_in"] = f(inp["w_in"][sl])
    m["w_ps"] = f(inp["w_proj_ssm"][sl])
    m["w_pp"] = f(inp["w_proj_pool"][sl])
    m["w_pb"] = f(inp["w_proj_sb"][sl])
    m["w_out"] = f(inp["w_out"][sl])
    m["pool_w"] = f(inp["pool_w"][sl])
    rep = lambda a: f(np.broadcast_to(np.asarray(a)[:, None, :], (NL, 128, np.asarray(a).shape[-1])))
    col = lambda a, n: f(np.asarray(a).reshape(NL, n, 128).transpose(0, 2, 1))
    m["p_normw"] = rep(inp["norm_w"][sl])
    cw = np.asarray(inp["conv_w"][sl])
    m["p_convw"] = f(cw.transpose(0, 2, 1).reshape(NL, 20, 128, 4).transpose(0, 2, 1, 3).reshape(NL, 128, 80))
    m["p_convb"] = col(inp["conv_b"][sl], 20)
    m["p_dtb"] = rep(inp["dt_bias"][sl])
    m["p_alog"] = rep(inp["a_log"][sl])
    m["p_dskip"] = col(np.repeat(np.asarray(inp["d_skip"][sl]), 64, axis=1), 16)
    m["p_ssmnw"] = col(inp["ssm_norm_w"][sl], 16)
    m["p_pscale"] = col(inp["pool_scale"][sl], 8)
    m["p_fnw"] = f(np.broadcast_to(np.asarray(inp["final_norm_w"])[None, :], (128, D)))
    m.update(host_consts())
    return m


_CACHE = {}


def kernel(**inputs):
    x = np.asarray(inputs["x"], dtype=np.float32)
    Bsz, S, _ = x.shape
    NL = inputs["w_in"].shape[0]
    NCORE = 8
    NSEQ = Bsz // NCORE
    key = (S, NSEQ, NL)
    if key not in _CACHE:
        _CACHE[key] = build(S, NSEQ, NL, True)[0]
    nc = _CACHE[key]
    shared = _prep_inputs(inputs, NL)
    in_maps = []
    for c in range(NCORE):
        mp = dict(shared)
        mp["x"] = np.ascontiguousarray(x[c * NSEQ:(c + 1) * NSEQ].reshape(NSEQ * S, D))
        in_maps.append(mp)
    res = run_bass_kernel_spmd(nc, in_maps, core_ids=list(range(NCORE)))
    out = np.concatenate([r["out"].reshape(NSEQ, S, D) for r in res.results], axis=0)
    return out.astype(np.float32)
```
